# Optimizing a Trainium2 kernel written in Bass

```python
import jax, jax.numpy as jnp
from jax import lax
import numpy as np

D_MODEL = 1024
BATCH = 8
SEQ = 4096
DEPTH = 4

N_META = 16
EPS = 1e-6
N_BRANCH = 4
POOL_WINDOWS = (2, 4, 8, 16)
POOL_GROUP = 64
POOL_W = POOL_GROUP * 4
MLA_HEADS = 8
QK_NOPE = 64
QK_ROPE = 32
V_DIM = 64
Q_RANK = 256
KV_RANK = 128
ROPE_THETA = 10000.0
MLA_W = MLA_HEADS * V_DIM
Q_BLOCK = 128
CONF_W = 256
CONF_K = 31
SC_W = 256
SC_K = 3

IN_SPLITS = (POOL_W, POOL_W,
             Q_RANK, KV_RANK, QK_ROPE, MLA_W,
             2 * CONF_W, CONF_W,
             3 * SC_W, SC_W,
             N_BRANCH * D_MODEL)
IN_W = sum(IN_SPLITS)

kernel_name = "hybrid_parallel_gated_mixers"


def rms_norm(x, g):
    xf = x.astype(jnp.float32)
    y = xf * lax.rsqrt(jnp.mean(xf * xf, axis=-1, keepdims=True) + EPS)
    return (y * g.astype(jnp.float32)).astype(x.dtype)


def layer_norm(x, g, b):
    xf = x.astype(jnp.float32)
    mu = jnp.mean(xf, axis=-1, keepdims=True)
    var = jnp.mean(jnp.square(xf - mu), axis=-1, keepdims=True)
    y = (xf - mu) * lax.rsqrt(var + EPS)
    return (y * g.astype(jnp.float32) + b.astype(jnp.float32)).astype(x.dtype)


def split_cols(z):
    idx = [int(i) for i in np.cumsum(IN_SPLITS)[:-1]]
    return jnp.split(z, idx, axis=-1)


def causal_dwconv(u, w):
    width, c = w.shape
    up = jnp.pad(u, ((0, 0), (width - 1, 0), (0, 0)))
    return lax.conv_general_dilated(up, w[:, None, :].astype(u.dtype), window_strides=(1,),
                                    padding='VALID', dimension_numbers=('NWC', 'WIO', 'NWC'),
                                    feature_group_count=c)


def rope_tables(n_pos, dim, dtype):
    inv = 1.0 / (ROPE_THETA ** (jnp.arange(0, dim, 2, dtype=jnp.float32) / dim))
    ang = jnp.arange(n_pos, dtype=jnp.float32)[:, None] * inv[None, :]
    return jnp.cos(ang).astype(dtype), jnp.sin(ang).astype(dtype)


def apply_rope(t, cos, sin):
    t1, t2 = jnp.split(t, 2, axis=-1)
    return jnp.concatenate([t1 * cos - t2 * sin, t1 * sin + t2 * cos], axis=-1)


def pool_mixer(v, w_group, scale):
    b_, l_, _ = v.shape
    vf = v.astype(jnp.float32)
    groups = jnp.split(vf, len(POOL_WINDOWS), axis=-1)
    pos = jnp.arange(l_)
    outs = []
    for g, w in zip(groups, POOL_WINDOWS):
        cs = jnp.cumsum(g, axis=1)
        lag = jnp.pad(cs, ((0, 0), (w, 0), (0, 0)))[:, :l_]
        cnt = jnp.minimum(pos + 1, w).astype(jnp.float32)[None, :, None]
        outs.append((cs - lag) / cnt - g)
    p = jnp.stack(outs, axis=2).astype(v.dtype)
    y = jnp.einsum('blgc,gcd->blgd', p, w_group).reshape(b_, l_, POOL_W)
    return y * scale


def mla_attention(c_q, c_kv, k_rope, q_norm_g, w_uq, kv_norm_g, w_ukv, cos, sin):
    b_, l_, _ = c_q.shape
    q = (rms_norm(c_q, q_norm_g) @ w_uq).reshape(b_, l_, MLA_HEADS, QK_NOPE + QK_ROPE)
    q_nope, q_rope = jnp.split(q, [QK_NOPE], axis=-1)
    q_rope = apply_rope(q_rope, cos[:, None, :], sin[:, None, :])
    kv = (rms_norm(c_kv, kv_norm_g) @ w_ukv).reshape(b_, l_, MLA_HEADS, QK_NOPE + V_DIM)
    k_nope, v = jnp.split(kv, [QK_NOPE], axis=-1)
    k_rope = apply_rope(k_rope, cos, sin)
    k = jnp.concatenate([k_nope, jnp.broadcast_to(k_rope[:, :, None, :], (b_, l_, MLA_HEADS, QK_ROPE))], axis=-1)
    qf = jnp.concatenate([q_nope, q_rope], axis=-1) * ((QK_NOPE + QK_ROPE) ** -0.5)
    n_blk = -(-l_ // Q_BLOCK)
    lp = n_blk * Q_BLOCK
    pad = ((0, 0), (0, lp - l_), (0, 0), (0, 0))
    qf, k, v = jnp.pad(qf, pad), jnp.pad(k, pad), jnp.pad(v, pad)
    k_pos = jnp.arange(lp)
    q_blocks = qf.reshape(b_, n_blk, Q_BLOCK, MLA_HEADS, QK_NOPE + QK_ROPE).transpose(1, 0, 2, 3, 4)

    def attend(args):
        qb, i = args
        s = jnp.einsum('bqhd,bkhd->bhqk', qb, k).astype(jnp.float32)
        q_pos = i * Q_BLOCK + jnp.arange(Q_BLOCK)
        s = jnp.where(k_pos[None, :] <= q_pos[:, None], s, -jnp.inf)
        p = jax.nn.softmax(s, axis=-1).astype(v.dtype)
        return jnp.einsum('bhqk,bkhd->bqhd', p, v)

    o = lax.map(attend, (q_blocks, jnp.arange(n_blk)))
    return o.transpose(1, 0, 2, 3, 4).reshape(b_, lp, MLA_W)[:, :l_]


def conformer_conv(u, w_dw, b_dw, ln_g, ln_b):
    a, gate = jnp.split(u, 2, axis=-1)
    z = a * jax.nn.sigmoid(gate)
    z = causal_dwconv(z, w_dw) + b_dw
    z = layer_norm(z, ln_g, ln_b)
    return jax.nn.silu(z)


def short_conv(bcx, w_dw):
    bg, cg, xv = jnp.split(bcx, 3, axis=-1)
    return bg * causal_dwconv(cg * xv, w_dw)


def setup_inputs(seed: int = 0) -> dict:
    key = jax.random.key(seed)
    ks = jax.random.split(key, 24)
    f32 = jnp.float32

    def nrm(k, shape, fan_in):
        return jax.random.normal(k, shape, f32) * (fan_in ** -0.5)

    def gain(k, shape):
        return 1.0 + 0.05 * jax.random.normal(k, shape, f32)

    def bias(k, shape):
        return 0.02 * jax.random.normal(k, shape, f32)

    return {
        "x": jax.random.normal(ks[0], (BATCH, SEQ, D_MODEL), f32),
        "meta_tokens": jax.random.normal(ks[1], (N_META, D_MODEL), f32),
        "pre_norm_g": gain(ks[2], (DEPTH, D_MODEL)),
        "w_in": nrm(ks[3], (DEPTH, D_MODEL, IN_W), D_MODEL),
        "gate_bias": bias(ks[4], (DEPTH, N_BRANCH * D_MODEL)),
        "pool_w": nrm(ks[5], (DEPTH, 4, POOL_GROUP, POOL_GROUP), POOL_GROUP),
        "pool_scale": gain(ks[6], (DEPTH, POOL_W)),
        "w_out_pool": nrm(ks[7], (DEPTH, POOL_W, D_MODEL), POOL_W),
        "q_norm_g": gain(ks[8], (DEPTH, Q_RANK)),
        "w_uq": nrm(ks[9], (DEPTH, Q_RANK, MLA_HEADS * (QK_NOPE + QK_ROPE)), Q_RANK),
        "kv_norm_g": gain(ks[10], (DEPTH, KV_RANK)),
        "w_ukv": nrm(ks[11], (DEPTH, KV_RANK, MLA_HEADS * (QK_NOPE + V_DIM)), KV_RANK),
        "w_out_mla": nrm(ks[12], (DEPTH, MLA_W, D_MODEL), MLA_W),
        "conf_dw_w": nrm(ks[13], (DEPTH, CONF_K, CONF_W), CONF_K),
        "conf_dw_b": bias(ks[14], (DEPTH, CONF_W)),
        "conf_ln_g": gain(ks[15], (DEPTH, CONF_W)),
        "conf_ln_b": bias(ks[16], (DEPTH, CONF_W)),
        "w_out_conf": nrm(ks[17], (DEPTH, CONF_W, D_MODEL), CONF_W),
        "sc_dw_w": nrm(ks[18], (DEPTH, SC_K, SC_W), SC_K),
        "w_out_sc": nrm(ks[19], (DEPTH, SC_W, D_MODEL), SC_W),
        "w_o": nrm(ks[20], (DEPTH, D_MODEL, D_MODEL), D_MODEL),
        "post_norm_g": gain(ks[21], (DEPTH, D_MODEL)),
    }


def reference(x, meta_tokens, pre_norm_g, w_in, gate_bias, pool_w, pool_scale, w_out_pool,
              q_norm_g, w_uq, kv_norm_g, w_ukv, w_out_mla, conf_dw_w, conf_dw_b, conf_ln_g,
              conf_ln_b, w_out_conf, sc_dw_w, w_out_sc, w_o, post_norm_g):
    b_ = x.shape[0]
    meta = jnp.broadcast_to(meta_tokens[None].astype(x.dtype), (b_, N_META, D_MODEL))
    h_res = jnp.concatenate([meta, x], axis=1)
    l_ = h_res.shape[1]
    cos, sin = rope_tables(l_, QK_ROPE, x.dtype)

    for i in range(DEPTH):
        h = rms_norm(h_res, pre_norm_g[i])
        z = h @ w_in[i]
        (pv, pg, cq, ckv, kr, mg, cu, cg, sbcx, sg, gl) = split_cols(z)

        y_a = (pool_mixer(pv, pool_w[i], pool_scale[i]) * jax.nn.silu(pg)) @ w_out_pool[i]
        y_b = (mla_attention(cq, ckv, kr, q_norm_g[i], w_uq[i], kv_norm_g[i], w_ukv[i], cos, sin)
               * jax.nn.silu(mg)) @ w_out_mla[i]
        y_c = (conformer_conv(cu, conf_dw_w[i], conf_dw_b[i], conf_ln_g[i], conf_ln_b[i])
               * jax.nn.silu(cg)) @ w_out_conf[i]
        y_d = (short_conv(sbcx, sc_dw_w[i]) * jax.nn.silu(sg)) @ w_out_sc[i]

        gts = jax.nn.sigmoid(gl + gate_bias[i]).reshape(b_, l_, N_BRANCH, D_MODEL)
        m = (gts[:, :, 0] * y_a + gts[:, :, 1] * y_b + gts[:, :, 2] * y_c + gts[:, :, 3] * y_d)
        h_res = h_res + rms_norm(m @ w_o[i], post_norm_g[i])

    return h_res[:, N_META:]
```

```python
import numpy as np
from contextlib import ExitStack
import concourse.bass as bass
import concourse.mybir as mybir
from concourse.bass_utils import run_bass_kernel_spmd

F32 = mybir.dt.float32
BF16 = mybir.dt.bfloat16
AF = mybir.ActivationFunctionType
ALU = mybir.AluOpType

D = 1024
NL = 4
SEQ = 4096
NMETA = 16
LT = SEQ + NMETA
EPS = 1e-6
NVC = 59
NU = 28
UW = 1280
NCOLL = 59
SCALE = 96 ** -0.5

U_UKV = 0
U_UQ = 1
U_CF = 4
U_SC = 11
U_OP = 12
U_WO = 20
C_GPRE, C_GPOST, C_GBIAS, C_PSC, C_QG, C_KVG, C_CFB, C_LNG, C_LNB = 0, 8, 16, 48, 50, 52, 53, 55, 57

VC_PV, VC_PG, VC_CQ, VC_CKV, VC_KRA, VC_KRB, VC_MG = 0, 2, 4, 6, 7, 8, 9
VC_CA, VC_CGATE, VC_CG, VC_BG, VC_CGS, VC_XV, VC_SG, VC_GL = 13, 15, 17, 19, 21, 23, 25, 27


class Lane:
    def __init__(s, name, waitall=False):
        s.name = name
        s.count = 0
        s.sem = None
        s.waitall = waitall


class T:
    __slots__ = ("name", "w", "r", "lane")

    def __init__(s, name="", lane=None):
        s.name = name
        s.w = None
        s.r = {}
        s.lane = lane


class Op:
    __slots__ = ("eng", "idx", "fn", "deps", "signal", "seq", "lane", "lseq", "waits")


COMPUTE = ("pe", "act", "dve", "pool")
ENGS = ("pe", "act", "dve", "pool", "sp")


class Prog:
    def __init__(s):
        s.ops = {e: [] for e in ENGS}
        s.lanes = []

    def lane(s, name, waitall=False):
        l = Lane(name, waitall)
        s.lanes.append(l)
        return l

    def add(s, eng, fn, reads=(), writes=(), lane=None):
        o = Op()
        o.eng = eng
        o.idx = len(s.ops[eng])
        o.fn = fn
        o.signal = False
        o.seq = 0
        o.lane = lane
        o.waits = None
        if lane is not None:
            lane.count += 1
            o.lseq = lane.count
        else:
            o.lseq = 0
        deps = []
        for t in reads:
            if t.w is not None:
                deps.append((t.w, True))
        for t in writes:
            if t.w is not None:
                deps.append((t.w, False))
            for d in t.r.values():
                deps.append((d, False))
        key = lane if lane is not None else eng
        for t in reads:
            t.r[key] = o
        for t in writes:
            t.w = o
            t.r = {}
        o.deps = deps
        s.ops[eng].append(o)
        return o

    def finalize(s, semctx):
        for eng in ENGS:
            waited = {}
            for o in s.ops[eng]:
                need = {}
                for d, raw in o.deps:
                    if d is o:
                        continue
                    if d.lane is not None:
                        if d.lane.waitall and d.lane is o.lane:
                            continue
                        k = d.lane
                        v = d.lseq
                    else:
                        if d.eng == eng and o.lane is None:
                            if eng == "pe" or not raw:
                                continue
                        k = d.eng
                        v = d.idx
                    if waited.get(k, -1) >= v:
                        continue
                    if k not in need or need[k][0] < v:
                        need[k] = (v, d)
                o.waits = []
                for k, (v, d) in need.items():
                    waited[k] = v
                    if d.lane is None:
                        d.signal = True
                    o.waits.append(d)
                o.deps = None
        for eng in COMPUTE:
            c = 0
            for o in s.ops[eng]:
                if o.signal:
                    c += 1
                    o.seq = c
        s.sems = {}
        for eng in COMPUTE:
            s.sems[eng] = semctx("e_" + eng)
        for l in s.lanes:
            if l.count > 0:
                l.sem = semctx("l_" + l.name)

    def emit(s, eng, e):
        for o in s.ops[eng]:
            for d in o.waits:
                if d.lane is not None:
                    v = 16 * (d.lane.count if d.lane.waitall else d.lseq)
                    e.wait_ge(d.lane.sem, v)
                else:
                    e.wait_ge(s.sems[d.eng], d.seq)
            ins = o.fn(e)
            if o.lane is not None:
                ins.then_inc(o.lane.sem, 16)
            elif o.signal:
                ins.then_inc(s.sems[eng], 1)
        if eng == "sp":
            for l in s.lanes:
                if l.count > 0:
                    e.wait_ge(l.sem, 16 * l.count)


def build_nc(n_layers=NL, debug=False, groups=None):
    nc = bass.Bass("TRN2", target_bir_lowering=False)
    P = Prog()
    if groups is None:
        groups = [(g * 512, 512) for g in range(8)] + [(4096, 16)]
    NG = len(groups)

    xin_d = nc.dram_tensor("xin", [D, LT], F32, kind="ExternalInput").ap()
    win_f = nc.dram_tensor("win_f", [NL * 128, NVC * 1024], F32, kind="ExternalInput").ap()
    wun_f = nc.dram_tensor("wun_f", [NL * 128, NU * UW], F32, kind="ExternalInput").ap()
    cols_d = nc.dram_tensor("cols", [128, NL * NCOLL], F32, kind="ExternalInput").ap()
    cbf_d = nc.dram_tensor("cbf", [128, 384], F32, kind="ExternalInput").ap()
    cf32_d = nc.dram_tensor("cf32", [128, 160], F32, kind="ExternalInput").ap()
    cs_d = nc.dram_tensor("cs", [128, 2 * LT], F32, kind="ExternalInput").ap()
    y_d = nc.dram_tensor("y", [D, LT], F32, kind="ExternalOutput").ap()
    win_b = nc.dram_tensor("win_b", [NL * 128, NVC * 1024], BF16, kind="Internal").ap()
    wun_b = nc.dram_tensor("wun_b", [NL * 128, NU * UW], BF16, kind="Internal").ap()
    dbg_d = None
    if debug:
        dbg_d = nc.dram_tensor("dbg", [NG, 128, 18 * 512], BF16, kind="ExternalOutput").ap()

    with ExitStack() as es:
        def sb(name, shape, dt):
            return es.enter_context(nc.sbuf_tensor(name, shape, dt))

        K_all = sb("K_all", [128, 8 * LT], BF16)
        V_all = sb("V_all", [128, 33 * 520], BF16)
        NVS = 5
        NUS = 3
        vring = sb("vring", [128, NVS * 1024], BF16)
        uring = sb("uring", [128, NUS * UW], BF16)
        hT = sb("hT", [128, 8 * 512], BF16)
        NF = 14
        NB = 30
        ftm = sb("ftm", [128, NF * 512], F32)
        btm = sb("btm", [128, NB * 512], BF16)
        pvbuf = sb("pvbuf", [128, 2 * 528], F32)
        pta = sb("pta", [128, 528], F32)
        ptb = sb("ptb", [128, 528], F32)
        glu = sb("glu", [128, 2 * 544], BF16)
        prodb = sb("prodb", [128, 2 * 520], BF16)
        cs = sb("cs_sb", [128, 1024], F32)
        cols = sb("cols_sb", [128, NL * NCOLL], F32)
        cbf_f = sb("cbf_f", [128, 384], F32)
        cbf = sb("cbf_b", [128, 384], BF16)
        cf32 = sb("cf32_sb", [128, 160], F32)
        pb = [es.enter_context(nc.psum_tensor(f"pb{i}", [128, 512], F32)) for i in range(8)]

        ident = cbf[:, 0:128]
        ones = cbf[:, 128:256]
        sel = cf32[:, 0:128]

        tKh = [[T(f"K{g}_{h}") for h in range(8)] for g in range(NG)]
        tKr = [T(f"Kr{g}") for g in range(NG)]
        tV = [[T(f"V{g}_{j}") for j in range(4)] for g in range(NG)]
        tVR = [T(f"vring{i}", P.lane(f"vring{i}")) for i in range(NVS)]
        tUR = [T(f"uring{i}", P.lane(f"uring{i}")) for i in range(NUS)]
        tH = T("hT")
        tF = [T(f"ft{i}", P.lane(f"ft{i}")) for i in range(NF)]
        tB = [T(f"bt{i}", P.lane(f"bt{i}")) for i in range(NB)]
        tPv, tPta, tPtb, tGlu, tProd = T("pv"), T("pta"), T("ptb"), T("glu"), T("prod")
        tCs = T("cs", P.lane("cs"))
        tConst = T("const")
        tPb = [T(f"pb{i}") for i in range(8)]
        tX = [[T(f"X{g}_{c}") for c in range(8)] for g in range(NG)]
        Linit = P.lane("init", waitall=True)
        Ldbg = P.lane("dbg", waitall=True)

        def ft(i, a=0, b=512, p0=0, p1=128):
            return ftm[p0:p1, i * 512 + a:i * 512 + b]

        def bt(i, a=0, b=512, p0=0, p1=128):
            return btm[p0:p1, i * 512 + a:i * 512 + b]

        btm_f = btm.bitcast(F32)

        def fa(j, a=0, b=512, p0=0, p1=128):
            return btm_f[p0:p1, j * 512 + a:j * 512 + b]

        def tA(j):
            return [tB[2 * j], tB[2 * j + 1]]

        def col(l, c):
            return cols[:, l * NCOLL + c:l * NCOLL + c + 1]

        B_Q = 0
        B_U = 8
        B_G = 18
        B_SQ = 22
        B_CQN = 25
        B_CKVN = 27
        B_P = 28
        F_O = 0
        F_RPRE = 8
        F_X = 10

        cnt = {"gb": 0, "ring": 0, "uring": 0, "z": 0, "g": 0, "sq": 0, "x": 0, "p": 0, "f": 0, "s": 0}

        ST = 2
        GB_ALL = [0, 1, 3, 4, 5]
        GB_ATT = [0, 1]
        gstate = {"att": False}

        def gbank():
            lst = GB_ATT if gstate["att"] else GB_ALL
            i = lst[cnt["gb"] % len(lst)]
            cnt["gb"] += 1
            return i

        def rot(key, base, n):
            i = base + cnt[key] % n
            cnt[key] += 1
            return i

        P.add("sp", lambda e: e.dma_start(out=cols[:], in_=cols_d), [], [tConst], lane=Linit)
        P.add("sp", lambda e: e.dma_start(out=cbf_f[:], in_=cbf_d), [], [tConst], lane=Linit)
        P.add("sp", lambda e: e.dma_start(out=cf32[:], in_=cf32_d), [], [tConst], lane=Linit)
        P.add("dve", lambda e: e.tensor_copy(out=cbf[:], in_=cbf_f[:]), [tConst], [tConst])
        P.add("pool", lambda e: e.memset(V_all[:], 1.0), [], [t for tv in tV for t in tv])
        for g_, (t0_, W_) in enumerate(groups):
            lx = P.lane(f"xcp{g_}")
            P.add("sp", lambda e, t0_=t0_, W_=W_: e.dma_start(out=y_d[:, t0_:t0_ + W_], in_=xin_d[:, t0_:t0_ + W_]),
                  [], tX[g_], lane=lx)
        NWP = 15
        NUP = 4
        tWinP = [T(f"winp{j}") for j in range(NWP)]
        tWunP = [T(f"wunp{j}") for j in range(NUP)]
        tWinC = [[T(f"winc{l}_{v}") for v in range(NVC)] for l in range(NL)]
        tWunC = [[T(f"wunc{l}_{u}") for u in range(NU)] for l in range(NL)]
        tChain = T("castchain")
        Lchain = P.lane("castchain")
        cast_q = []
        LWinP = [P.lane(f"cwi{j}") for j in range(NWP)]
        LWunP = [P.lane(f"cwu{j}") for j in range(NUP)]
        cast_order = [("i", 1), ("i", 2), ("i", 0), ("u", 0), ("i", 6), ("i", 5), ("i", 4), ("i", 3), ("u", 1), ("u", 2),
                      ("i", 7), ("i", 8), ("i", 9), ("u", 3), ("i", 10), ("i", 11), ("i", 12), ("i", 13), ("i", 14)]

        def emit_cast(l, kind, j):
            if kind == "i":
                c0_ = j * 4096
                c1_ = min((j + 1) * 4096, NVC * 1024)
                P.add("pool", lambda e: e.dma_start(out=win_b[l * 128:(l + 1) * 128, c0_:c1_], in_=win_f[l * 128:(l + 1) * 128, c0_:c1_]),
                      [], [tWinP[j]], lane=LWinP[j])
            else:
                c0_ = j * 7 * UW
                c1_ = (j + 1) * 7 * UW
                P.add("pool", lambda e: e.dma_start(out=wun_b[l * 128:(l + 1) * 128, c0_:c1_], in_=wun_f[l * 128:(l + 1) * 128, c0_:c1_]),
                      [], [tWunP[j]], lane=LWunP[j])

        def emit_cast_chunk(l, kind, i):
            if kind == "i":
                P.add("pool", lambda e: e.dma_start(out=win_b[l * 128:(l + 1) * 128, i * 1024:(i + 1) * 1024],
                                                    in_=win_f[l * 128:(l + 1) * 128, i * 1024:(i + 1) * 1024]),
                      [tChain], [tChain, tWinC[l][i]], lane=Lchain)
            else:
                P.add("pool", lambda e: e.dma_start(out=wun_b[l * 128:(l + 1) * 128, i * UW:(i + 1) * UW],
                                                    in_=wun_f[l * 128:(l + 1) * 128, i * UW:(i + 1) * UW]),
                      [tChain], [tChain, tWunC[l][i]], lane=Lchain)

        for (kind, j) in cast_order:
            emit_cast(0, kind, j)

        def ring_load_vc(l, vc):
            s_ = cnt["ring"] % NVS
            cnt["ring"] += 1
            P.add("sp", lambda e: e.dma_start(out=vring[:, s_ * 1024:(s_ + 1) * 1024],
                                              in_=win_b[l * 128:(l + 1) * 128, vc * 1024:(vc + 1) * 1024]),
                  [tWinP[vc // 4] if l == 0 else tWinC[l][vc]], [tVR[s_]], lane=tVR[s_].lane)
            return s_

        def ring_load_unit(l, u):
            s_ = cnt["uring"] % NUS
            cnt["uring"] += 1
            uw_ = 1024 if u >= U_WO else UW
            P.add("sp", lambda e: e.dma_start(out=uring[:, s_ * UW:s_ * UW + uw_],
                                              in_=wun_b[l * 128:(l + 1) * 128, u * UW:u * UW + uw_]),
                  [tWunP[u // 7] if l == 0 else tWunC[l][u]], [tUR[s_]], lane=tUR[s_].lane)
            return s_

        def zmm(l, vc, W):
            cnt["z"] += 1
            if cast_q and cnt["z"] % 5 == 0:
                emit_cast_chunk(*cast_q.pop(0))
            s_ = ring_load_vc(l, vc)
            b = gbank()
            for dc in range(8):
                P.add("pe", lambda e, dc=dc: e.matmul(pb[b][:, 0:W], vring[:, s_ * 1024 + dc * 128:s_ * 1024 + (dc + 1) * 128],
                                                      hT[:, dc * 512:dc * 512 + W], start=(dc == 0), stop=(dc == 7)),
                      [tVR[s_], tH], [tPb[b]])
            return b

        def x_src(l):
            return y_d

        LpoolF = {i: P.lane(f"ftp{i}") for i in range(NF)}

        def load_x(l, g, c, fi, eng="pool"):
            t0, W = groups[g]
            src = x_src(l)
            ln = tF[fi].lane if eng == "sp" else LpoolF[fi]
            P.add(eng, lambda e: e.dma_start(out=ft(fi, 0, W), in_=src[c * 128:(c + 1) * 128, t0:t0 + W]),
                  [tX[g][c]], [tF[fi]], lane=ln)

        def rstd_from_stat(bank, W, scale, fo):
            P.add("act", lambda e: e.activation(out=ft(fo, 0, W), in_=pb[bank][:, 0:W], func=AF.Ln, bias=EPS, scale=scale),
                  [tPb[bank]], [tF[fo]])
            P.add("act", lambda e: e.activation(out=ft(fo, 0, W), in_=ft(fo, 0, W), func=AF.Exp, scale=-0.5), [tF[fo]], [tF[fo]])

        def prenorm1(l, g):
            t0, W = groups[g]
            fr = F_RPRE + g % 2
            for c in range(8):
                fi = rot("x", F_X, 4)
                load_x(l, g, c, fi)
                bi = rot("sq", B_SQ, 3)
                P.add("act", lambda e, fi=fi, bi=bi: e.activation(out=bt(bi, 0, W), in_=ft(fi, 0, W), func=AF.Square),
                      [tF[fi]], [tB[bi]])
                P.add("pe", lambda e, bi=bi, c=c: e.matmul(pb[ST][:, 0:W], ones, bt(bi, 0, W), start=(c == 0), stop=(c == 7)),
                      [tB[bi], tConst], [tPb[ST]])
            rstd_from_stat(ST, W, 1.0 / D, fr)

        def make_hT(l, g):
            t0, W = groups[g]
            fr = F_RPRE + g % 2
            for c in range(8):
                fi = rot("x", F_X, 4)
                load_x(l, g, c, fi)
                P.add("dve", lambda e, fi=fi, c=c: e.scalar_tensor_tensor(
                    out=hT[:, c * 512:c * 512 + W], in0=ft(fi, 0, W), scalar=col(l, C_GPRE + c), in1=ft(fr, 0, W),
                    op0=ALU.mult, op1=ALU.mult), [tF[fi], tF[fr], tConst], [tH])

        pending_final = [None]
        LKr = P.lane("krope")

        def emit_final(l, g, chunks=range(8)):
            t0, W = groups[g]
            f_rp = F_X + 3
            for c in chunks:
                P.add("dve", lambda e, c=c: e.scalar_tensor_tensor(out=ft(F_O + c, 0, W), in0=ft(F_O + c, 0, W), scalar=col(l, C_GPOST + c),
                                                                   in1=ft(f_rp, 0, W), op0=ALU.mult, op1=ALU.mult),
                      [tF[F_O + c], tF[f_rp], tConst], [tF[F_O + c]])
                P.add("pool", lambda e, c=c: e.dma_start(out=y_d[c * 128:(c + 1) * 128, t0:t0 + W], in_=ft(F_O + c, 0, W), accum_op=ALU.add),
                      [tF[F_O + c]], [tX[g][c]], lane=LpoolF[F_O + c])

        def layer_group(l, g):
            t0, W = groups[g]
            n4 = (W + 127) // 128
            kt0 = t0 // 128
            P.add("sp", lambda e: e.dma_start(out=cs[:, 0:W], in_=cs_d[:, t0:t0 + W]), [], [tCs], lane=tCs.lane)
            P.add("sp", lambda e: e.dma_start(out=cs[:, 512:512 + W], in_=cs_d[:, LT + t0:LT + t0 + W]), [], [tCs], lane=tCs.lane)
            a_cq = [4, 5]
            a_rq, a_ckv, a_rkv, a_ka, a_kb = 6, 7, 8, 9, 10
            sq_cq = []
            for k in range(2):
                b = zmm(l, VC_CQ + k, W)
                sq = rot("sq", B_SQ, 3)
                P.add("act", lambda e, b=b, sq=sq: e.activation(out=bt(sq, 0, W), in_=pb[b][:, 0:W], func=AF.Square), [tPb[b]], [tB[sq]])
                P.add("act", lambda e, b=b, k=k: e.copy(out=fa(a_cq[k], 0, W), in_=pb[b][:, 0:W]), [tPb[b]], tA(a_cq[k]))
                sq_cq.append(sq)
            b = zmm(l, VC_CKV, W)
            sq_kv = rot("sq", B_SQ, 3)
            P.add("act", lambda e, b=b: e.activation(out=bt(sq_kv, 0, W), in_=pb[b][:, 0:W], func=AF.Square), [tPb[b]], [tB[sq_kv]])
            P.add("act", lambda e, b=b: e.copy(out=fa(a_ckv, 0, W), in_=pb[b][:, 0:W]), [tPb[b]], tA(a_ckv))
            ba = zmm(l, VC_KRA, W)
            P.add("dve", lambda e, ba=ba: e.tensor_tensor(out=fa(a_ka, 0, W, 64, 128), in0=pb[ba][64:128, 0:W], in1=cs[64:128, 0:W], op=ALU.mult),
                  [tPb[ba], tCs], tA(a_ka))
            bb = zmm(l, VC_KRB, W)
            P.add("dve", lambda e, bb=bb: e.tensor_tensor(out=fa(a_kb, 0, W, 64, 128), in0=pb[bb][64:128, 0:W], in1=cs[64:128, 512:512 + W], op=ALU.mult),
                  [tPb[bb], tCs], tA(a_kb))
            P.add("pool", lambda e: e.tensor_tensor(out=K_all[64:128, t0:t0 + W], in0=fa(a_ka, 0, W, 64, 128), in1=fa(a_kb, 0, W, 64, 128), op=ALU.add),
                  tA(a_ka) + tA(a_kb), [tKr[g]])
            P.add("pool", lambda e: e.dma_start(out=bass.AP(K_all, 64 * 8 * LT + LT + t0, [[8 * LT, 64], [LT, 7], [1, W]]),
                                              in_=bass.AP(K_all, 64 * 8 * LT + t0, [[8 * LT, 64], [0, 7], [1, W]])),
                  [tKr[g]], [tKr[g]], lane=LKr)
            if pending_final[0] is not None:
                emit_final(pending_final[0][0], pending_final[0][1], [4, 5])
            f_sg = []
            for k in range(2):
                b = zmm(l, VC_CGATE + k, W)
                fi = F_O + 4 + k
                P.add("act", lambda e, b=b, fi=fi: e.activation(out=ft(fi, 0, W), in_=pb[b][:, 0:W], func=AF.Sigmoid),
                      [tPb[b]], [tF[fi]])
                f_sg.append(fi)
            for k in range(2):
                b = zmm(l, VC_CA + k, W)
                P.add("dve", lambda e, b=b, k=k: e.tensor_tensor(out=glu[:, k * 544 + 32:k * 544 + 32 + W], in0=pb[b][:, 0:W],
                                                                 in1=ft(f_sg[k], 0, W), op=ALU.mult),
                      [tPb[b], tF[f_sg[k]]], [tGlu])
            for k in range(2):
                b = zmm(l, VC_PV + k, W)
                P.add("act", lambda e, b=b, k=k: e.copy(out=pvbuf[:, k * 528 + 16:k * 528 + 16 + W], in_=pb[b][:, 0:W]),
                      [tPb[b]], [tPv])
            g_pg = []
            for k in range(2):
                b = zmm(l, VC_PG + k, W)
                gi = B_G + k
                P.add("act", lambda e, b=b, gi=gi: e.activation(out=bt(gi, 0, W), in_=pb[b][:, 0:W], func=AF.Silu),
                      [tPb[b]], [tB[gi]])
                g_pg.append(gi)
            p_tiles = []
            for k in range(2):
                vo = k * 528
                NW = 16 + W
                P.add("pool", lambda e, vo=vo, NW=NW: e.tensor_tensor(out=pta[:, 1:NW], in0=pvbuf[:, vo:vo + NW - 1],
                                                                       in1=pvbuf[:, vo + 1:vo + NW], op=ALU.add),
                      [tPv], [tPta])
                pi = B_P + k
                if k == 0:
                    P.add("pool", lambda e, NW=NW: e.tensor_tensor(out=ptb[64:128, 3:NW], in0=pta[64:128, 1:NW - 2],
                                                                    in1=pta[64:128, 3:NW], op=ALU.add), [tPta], [tPtb])
                    srcs = [(pta, 0, 64, 0.5), (ptb, 64, 128, 0.25)]
                else:
                    P.add("pool", lambda e, NW=NW: e.tensor_tensor(out=ptb[:, 3:NW], in0=pta[:, 1:NW - 2],
                                                                    in1=pta[:, 3:NW], op=ALU.add), [tPta], [tPtb])
                    P.add("pool", lambda e, NW=NW: e.tensor_tensor(out=pta[:, 7:NW], in0=ptb[:, 3:NW - 4],
                                                                    in1=ptb[:, 7:NW], op=ALU.add), [tPtb], [tPta])
                    P.add("pool", lambda e, NW=NW: e.tensor_tensor(out=ptb[64:128, 15:NW], in0=pta[64:128, 7:NW - 8],
                                                                    in1=pta[64:128, 15:NW], op=ALU.add), [tPta], [tPtb])
                    srcs = [(pta, 0, 64, 0.125), (ptb, 64, 128, 0.0625)]
                for (sbuf_, p0, p1, iw) in srcs:
                    P.add("dve", lambda e, sbuf_=sbuf_, p0=p0, p1=p1, iw=iw, vo=vo, pi=pi: e.scalar_tensor_tensor(
                        out=bt(pi, 0, W, p0, p1), in0=sbuf_[p0:p1, 16:16 + W], scalar=iw, in1=pvbuf[p0:p1, vo + 16:vo + 16 + W],
                        op0=ALU.mult, op1=ALU.subtract), [tPta, tPtb, tPv], [tB[pi]])
                    if g == 0:
                        fx = F_X + 3
                        P.add("pool", lambda e, sbuf_=sbuf_, p0=p0, p1=p1, k=k, fx=fx: e.tensor_tensor(
                            out=ft(fx, 0, 16, p0, p1), in0=sbuf_[p0:p1, 16:32], in1=cf32[p0:p1, 128 + 16 * k:144 + 16 * k],
                            op=ALU.mult), [tPta, tPtb, tConst], [tF[fx]])
                        P.add("pool", lambda e, p0=p0, p1=p1, vo=vo, pi=pi, fx=fx: e.tensor_tensor(
                            out=bt(pi, 0, 16, p0, p1), in0=ft(fx, 0, 16, p0, p1), in1=pvbuf[p0:p1, vo + 16:vo + 32],
                            op=ALU.subtract), [tF[fx], tPv], [tB[pi]])
                p_tiles.append(pi)
            if W >= 16 and g + 1 < NG:
                for k in range(2):
                    P.add("pool", lambda e, k=k: e.tensor_copy(out=pvbuf[:, k * 528:k * 528 + 16],
                                                               in_=pvbuf[:, k * 528 + W:k * 528 + W + 16]), [tPv], [tPv])
            if pending_final[0] is not None:
                emit_final(pending_final[0][0], pending_final[0][1], [0, 1, 2, 3])
            g_bs = []
            for k in range(2):
                b = zmm(l, VC_SG + k, W)
                gi = B_G + 2 + k
                P.add("act", lambda e, b=b, gi=gi: e.activation(out=bt(gi, 0, W), in_=pb[b][:, 0:W], func=AF.Silu),
                      [tPb[b]], [tB[gi]])
                g_bs.append(gi)
            f_bs = []
            for k in range(2):
                b = zmm(l, VC_BG + k, W)
                fi = F_O + k
                P.add("dve", lambda e, b=b, fi=fi, k=k: e.tensor_tensor(out=ft(fi, 0, W), in0=pb[b][:, 0:W], in1=bt(g_bs[k], 0, W),
                                                                        op=ALU.mult), [tPb[b], tB[g_bs[k]]], [tF[fi]])
                f_bs.append(fi)
            f_cgs = []
            for k in range(2):
                b = zmm(l, VC_CGS + k, W)
                fi = F_O + 2 + k
                P.add("act", lambda e, b=b, fi=fi: e.copy(out=ft(fi, 0, W), in_=pb[b][:, 0:W]), [tPb[b]], [tF[fi]])
                f_cgs.append(fi)
            for k in range(2):
                b = zmm(l, VC_XV + k, W)
                P.add("dve", lambda e, b=b, k=k: e.tensor_tensor(out=prodb[:, k * 520 + 2:k * 520 + 2 + W], in0=pb[b][:, 0:W],
                                                                 in1=ft(f_cgs[k], 0, W), op=ALU.mult),
                      [tPb[b], tF[f_cgs[k]]], [tProd])
            if pending_final[0] is not None:
                emit_final(pending_final[0][0], pending_final[0][1], [6, 7])
            pending_final[0] = None
            for k in range(2):
                P.add("pe", lambda e, k=k: e.matmul(pb[ST][:, 0:W], ones, bt(sq_cq[k], 0, W), start=(k == 0), stop=(k == 1)),
                      [tB[sq_cq[k]], tConst], [tPb[ST]])
            P.add("act", lambda e: e.activation(out=fa(a_rq, 0, W), in_=pb[ST][:, 0:W], func=AF.Ln, bias=EPS, scale=1.0 / 256), [tPb[ST]], tA(a_rq))
            P.add("pe", lambda e: e.matmul(pb[6][:, 0:W], ones, bt(sq_kv, 0, W), start=True, stop=True), [tB[sq_kv], tConst], [tPb[6]])
            P.add("act", lambda e: e.activation(out=fa(a_rkv, 0, W), in_=pb[6][:, 0:W], func=AF.Ln, bias=EPS, scale=1.0 / 128), [tPb[6]], tA(a_rkv))
            P.add("act", lambda e: e.activation(out=fa(a_rq, 0, W), in_=fa(a_rq, 0, W), func=AF.Exp, scale=-0.5), tA(a_rq), tA(a_rq))
            P.add("act", lambda e: e.activation(out=fa(a_rkv, 0, W), in_=fa(a_rkv, 0, W), func=AF.Exp, scale=-0.5), tA(a_rkv), tA(a_rkv))
            for k in range(2):
                P.add("dve", lambda e, k=k: e.scalar_tensor_tensor(out=bt(B_CQN + k, 0, W), in0=fa(a_cq[k], 0, W), scalar=col(l, C_QG + k),
                                                                   in1=fa(a_rq, 0, W), op0=ALU.mult, op1=ALU.mult),
                      tA(a_cq[k]) + tA(a_rq) + [tConst], [tB[B_CQN + k]])
            P.add("dve", lambda e: e.scalar_tensor_tensor(out=bt(B_CKVN, 0, W), in0=fa(a_ckv, 0, W), scalar=col(l, C_KVG),
                                                          in1=fa(a_rkv, 0, W), op0=ALU.mult, op1=ALU.mult),
                  tA(a_ckv) + tA(a_rkv) + [tConst], [tB[B_CKVN]])
            cf_slots = {}
            f_zc = []
            b_zb = []
            b_zs = []
            for k in range(2):
                b = gbank()
                for j in range(31):
                    idx = k * 31 + j
                    u = idx // 10
                    if u not in cf_slots:
                        cf_slots[u] = ring_load_unit(l, U_CF + u)
                    s_ = cf_slots[u]
                    o_ = s_ * UW + (idx % 10) * 128
                    P.add("pe", lambda e, b=b, k=k, j=j, o_=o_: e.matmul(pb[b][:, 0:W], uring[:, o_:o_ + 128],
                                                                         glu[:, k * 544 + 2 + j:k * 544 + 2 + j + W], start=(j == 0), stop=(j == 30)),
                          [tUR[s_], tGlu], [tPb[b]])
                fi = F_O + 6 + k
                P.add("act", lambda e, b=b, fi=fi, k=k: e.activation(out=ft(fi, 0, W), in_=pb[b][:, 0:W], func=AF.Identity,
                                                                     bias=col(l, C_CFB + k), scale=1.0), [tPb[b], tConst], [tF[fi]])
                zb = rot("sq", B_SQ, 3)
                P.add("act", lambda e, b=b, zb=zb, k=k: e.activation(out=bt(zb, 0, W), in_=pb[b][:, 0:W], func=AF.Identity,
                                                                     bias=col(l, C_CFB + k), scale=1.0), [tPb[b], tConst], [tB[zb]])
                zs = B_G + 2 + k
                P.add("act", lambda e, b=b, zs=zs, k=k: e.activation(out=bt(zs, 0, W), in_=pb[b][:, 0:W], func=AF.Square,
                                                                     bias=col(l, C_CFB + k), scale=1.0), [tPb[b], tConst], [tB[zs]])
                f_zc.append(fi)
                b_zb.append(zb)
                b_zs.append(zs)
            if W >= 32 and g + 1 < NG:
                for k in range(2):
                    P.add("pool", lambda e, k=k: e.tensor_copy(out=glu[:, k * 544 + 2:k * 544 + 32], in_=glu[:, k * 544 + W + 2:k * 544 + W + 32]),
                          [tGlu], [tGlu])
            s_ukv2 = ring_load_unit(l, U_UKV)
            for h in range(8):
                b = gbank()
                P.add("pe", lambda e, b=b, h=h: e.matmul(pb[b][0:64, 0:W], uring[:, s_ukv2 * UW + h * 64:s_ukv2 * UW + (h + 1) * 64],
                                                         bt(B_CKVN, 0, W), start=True, stop=True), [tUR[s_ukv2], tB[B_CKVN]], [tPb[b]])
                if h % 2 == 0:
                    P.add("act", lambda e, b=b, h=h: e.copy(out=K_all[0:64, h * LT + t0:h * LT + t0 + W], in_=pb[b][0:64, 0:W]),
                          [tPb[b]], [tKh[g][h]])
                else:
                    P.add("dve", lambda e, b=b, h=h: e.tensor_copy(out=K_all[0:64, h * LT + t0:h * LT + t0 + W], in_=pb[b][0:64, 0:W]),
                          [tPb[b]], [tKh[g][h]])
            for j in range(n4):
                tw = min(128, W - 128 * j)
                b = gbank()
                P.add("pe", lambda e, b=b, j=j, tw=tw: e.matmul(pb[b][0:tw, 0:512], bt(B_CKVN, 128 * j, 128 * j + tw),
                                                                uring[:, s_ukv2 * UW + 512:s_ukv2 * UW + 1024], start=True, stop=True),
                      [tUR[s_ukv2], tB[B_CKVN]], [tPb[b]])
                kt = kt0 + j
                dst = bass.AP(V_all, kt * 520, [[33 * 520, tw], [65, 8], [1, 64]])
                src = bass.AP(pb[b], 0, [[512, tw], [64, 8], [1, 64]])
                if j % 2 == 0:
                    P.add("dve", lambda e, dst=dst, src=src: e.tensor_copy(out=dst, in_=src), [tPb[b]], [tV[g][j]])
                else:
                    P.add("act", lambda e, dst=dst, src=src: e.copy(out=dst, in_=src), [tPb[b]], [tV[g][j]])
            bS1 = 6
            bS2 = 7
            for k in range(2):
                P.add("pe", lambda e, k=k: e.matmul(pb[bS1][:, 0:W], ones, bt(b_zb[k], 0, W), start=(k == 0), stop=(k == 1)),
                      [tB[b_zb[k]], tConst], [tPb[bS1]])
            for k in range(2):
                P.add("pe", lambda e, k=k: e.matmul(pb[bS2][:, 0:W], ones, bt(b_zs[k], 0, W), start=(k == 0), stop=(k == 1)),
                      [tB[b_zs[k]], tConst], [tPb[bS2]])
            f_mu = F_O + 2
            f_t = F_O + 3
            f_rs = F_O + 4
            P.add("act", lambda e: e.mul(out=ft(f_mu, 0, W), in_=pb[bS1][:, 0:W], mul=1.0 / 256),
                  [tPb[bS1]], [tF[f_mu]])
            P.add("pool", lambda e: e.tensor_tensor(out=ft(f_t, 0, W), in0=ft(f_mu, 0, W), in1=ft(f_mu, 0, W), op=ALU.mult),
                  [tF[f_mu]], [tF[f_t]])
            P.add("dve", lambda e: e.scalar_tensor_tensor(out=ft(f_rs, 0, W), in0=pb[bS2][:, 0:W], scalar=1.0 / 256, in1=ft(f_t, 0, W),
                                                          op0=ALU.mult, op1=ALU.subtract), [tPb[bS2], tF[f_t]], [tF[f_rs]])
            for k in range(2):
                fi = f_zc[k]
                P.add("pool", lambda e, fi=fi: e.tensor_tensor(out=ft(fi, 0, W), in0=ft(fi, 0, W), in1=ft(f_mu, 0, W), op=ALU.subtract),
                      [tF[fi], tF[f_mu]], [tF[fi]])
            uq_slots = {}
            for h in range(8):
                u = h // 3
                if u not in uq_slots:
                    uq_slots[u] = ring_load_unit(l, U_UQ + u)
                s_ = uq_slots[u]
                o_ = s_ * UW + (h % 3) * 256
                bqa = gbank()
                for k in range(2):
                    P.add("pe", lambda e, k=k, bqa=bqa, o_=o_: e.matmul(pb[bqa][:, 0:W], uring[:, o_ + k * 128:o_ + (k + 1) * 128],
                                                                        bt(B_CQN + k, 0, W), start=(k == 0), stop=(k == 1)),
                          [tUR[s_], tB[B_CQN + k]], [tPb[bqa]])
                P.add("dve", lambda e, bqa=bqa, h=h: e.tensor_tensor(out=bt(B_Q + h, 0, W), in0=pb[bqa][:, 0:W], in1=cs[:, 0:W], op=ALU.mult),
                      [tPb[bqa], tCs], [tB[B_Q + h]])
            P.add("act", lambda e: e.activation(out=ft(f_rs, 0, W), in_=ft(f_rs, 0, W), func=AF.Ln, bias=EPS, scale=1.0),
                  [tF[f_rs]], [tF[f_rs]])
            P.add("act", lambda e: e.activation(out=ft(f_rs, 0, W), in_=ft(f_rs, 0, W), func=AF.Exp, scale=-0.5), [tF[f_rs]], [tF[f_rs]])
            for k in range(2):
                fi = f_zc[k]
                P.add("dve", lambda e, fi=fi: e.tensor_tensor(out=ft(fi, 0, W), in0=ft(fi, 0, W), in1=ft(f_rs, 0, W), op=ALU.mult),
                      [tF[fi], tF[f_rs]], [tF[fi]])
            s_ukv = ring_load_unit(l, U_UKV)
            for k in range(2):
                b = gbank()
                P.add("pe", lambda e, b=b, k=k: e.matmul(pb[b][:, 0:W], uring[:, s_ukv * UW + 1024 + k * 128:s_ukv * UW + 1152 + k * 128],
                                                         bt(p_tiles[k], 0, W), start=True, stop=True),
                      [tUR[s_ukv], tB[p_tiles[k]]], [tPb[b]])
                P.add("dve", lambda e, b=b, k=k: e.scalar_tensor_tensor(out=bt(B_U + k, 0, W), in0=pb[b][:, 0:W], scalar=col(l, C_PSC + k),
                                                                        in1=bt(g_pg[k], 0, W), op0=ALU.mult, op1=ALU.mult),
                      [tPb[b], tB[g_pg[k]], tConst], [tB[B_U + k]])
            s_sc = ring_load_unit(l, U_SC)
            for k in range(2):
                b = gbank()
                for j in range(3):
                    P.add("pe", lambda e, b=b, k=k, j=j: e.matmul(pb[b][:, 0:W], uring[:, s_sc * UW + (k * 3 + j) * 128:s_sc * UW + (k * 3 + j + 1) * 128],
                                                                  prodb[:, k * 520 + j:k * 520 + j + W], start=(j == 0), stop=(j == 2)),
                          [tUR[s_sc], tProd], [tPb[b]])
                P.add("dve", lambda e, b=b, k=k: e.tensor_tensor(out=bt(B_U + 8 + k, 0, W), in0=pb[b][:, 0:W], in1=ft(f_bs[k], 0, W), op=ALU.mult),
                      [tPb[b], tF[f_bs[k]]], [tB[B_U + 8 + k]])
            if W >= 2 and g + 1 < NG:
                for k in range(2):
                    P.add("pool", lambda e, k=k: e.tensor_copy(out=prodb[:, k * 520:k * 520 + 2], in_=prodb[:, k * 520 + W:k * 520 + W + 2]),
                          [tProd], [tProd])
            g_cg = []
            for k in range(2):
                b = zmm(l, VC_CG + k, W)
                gi = B_G + k
                P.add("act", lambda e, b=b, gi=gi: e.activation(out=bt(gi, 0, W), in_=pb[b][:, 0:W], func=AF.Silu),
                      [tPb[b]], [tB[gi]])
                g_cg.append(gi)
            for k in range(2):
                fi = f_zc[k]
                P.add("act", lambda e, fi=fi, k=k: e.activation(out=ft(fi, 0, W), in_=ft(fi, 0, W), func=AF.Silu,
                                                                bias=col(l, C_LNB + k), scale=col(l, C_LNG + k)), [tF[fi], tConst], [tF[fi]])
                P.add("dve", lambda e, fi=fi, k=k: e.tensor_tensor(out=bt(B_U + 6 + k, 0, W), in0=ft(fi, 0, W), in1=bt(g_cg[k], 0, W), op=ALU.mult),
                      [tF[fi], tB[g_cg[k]]], [tB[B_U + 6 + k]])
            g_mgs = [B_G + 0, B_P + 0, B_P + 1, B_CQN + 0]
            for p in range(4):
                b = zmm(l, VC_MG + p, W)
                P.add("act", lambda e, b=b, p=p: e.activation(out=bt(g_mgs[p], 0, W), in_=pb[b][:, 0:W], func=AF.Silu), [tPb[b]], [tB[g_mgs[p]]])
            if g + 1 < NG:
                prenorm1(l, g + 1)
            nkt = kt0 + n4
            gstate["att"] = True
            cnt["gb"] = 0
            pending = [None]
            f_x = F_O + 0
            f_u = F_O + 1
            f_r = F_O + 2

            def norm_tail(p):
                g_mg = g_mgs[p]
                br = gbank()
                P.add("pe", lambda e: e.matmul(pb[br][:, 0:W], sel, ft(f_x, 0, W), start=True, stop=True), [tF[f_x], tConst], [tPb[br]])
                P.add("dve", lambda e: e.reciprocal(out=ft(f_r, 0, W), in_=pb[br][:, 0:W]), [tPb[br]], [tF[f_r]])
                P.add("pool", lambda e: e.tensor_tensor(out=ft(f_u, 0, W), in0=ft(f_u, 0, W), in1=ft(f_r, 0, W), op=ALU.mult),
                      [tF[f_u], tF[f_r]], [tF[f_u]])
                P.add("pool", lambda e: e.tensor_tensor(out=bt(B_U + 2 + p, 0, W), in0=ft(f_u, 0, W), in1=bt(g_mg, 0, W), op=ALU.mult),
                      [tF[f_u], tB[g_mg]], [tB[B_U + 2 + p]])

            for p in range(4):
                items = [(hh, kt) for hh in range(2) for kt in range(nkt)]
                n_it = len(items)

                def geom(i):
                    hh, kt = items[i]
                    h = 2 * p + hh
                    if kt < kt0:
                        return h, hh, kt, 0, 128
                    c0 = 128 * (kt - kt0)
                    return h, hh, kt, c0, min(128, W - c0)

                def qk(i):
                    h, hh, kt, c0, kw = geom(i)
                    sbk = 3 + i % 3
                    kdeps = [tKh[gg][h] for gg in range(g + 1)] + tKr[:g + 1]
                    diag = kt >= kt0
                    P.add("pe", lambda e: e.matmul(pb[sbk][0:kw, c0:W], K_all[:, h * LT + kt * 128:h * LT + kt * 128 + kw],
                                                   bt(B_Q + h, c0, W), start=True, stop=(not diag)),
                          kdeps + [tB[B_Q + h]], [tPb[sbk]])
                    if diag:
                        P.add("pe", lambda e: e.matmul(pb[sbk][0:kw, c0:c0 + kw], cbf[0:kw, 0:kw], cbf[0:kw, 256:256 + kw], start=False, stop=True),
                              [tConst], [tPb[sbk]])
                    pi = B_G + 1 + i % 3
                    P.add("act", lambda e: e.activation(out=bt(pi, c0, W, 0, kw), in_=pb[sbk][0:kw, c0:W], func=AF.Exp, scale=SCALE),
                          [tPb[sbk]], [tB[pi]])

                def pv(i):
                    h, hh, kt, c0, kw = geom(i)
                    ob = 6 + hh
                    vs = h * 65 if hh == 0 else (h - 1) * 65 + 1
                    pi = B_G + 1 + i % 3
                    vdeps = [t for gg in range(g + 1) for t in tV[gg]]
                    P.add("pe", lambda e: e.matmul(pb[ob][:, c0:W], V_all[0:kw, kt * 520 + vs:kt * 520 + vs + 128], bt(pi, c0, W, 0, kw),
                                                   start=(kt == 0), stop=(kt == nkt - 1)), vdeps + [tB[pi]], [tPb[ob]])

                if W <= 16 and kt0 * W <= 512:
                    if pending[0] is not None:
                        norm_tail(pending[0])
                        pending[0] = None
                    for hh in range(2):
                        h = 2 * p + hh
                        sbk = 3 + hh
                        pi = B_G + 1 + hh
                        kdeps = [tKh[gg][h] for gg in range(g + 1)] + tKr[:g + 1]
                        for kt in range(kt0):
                            P.add("pe", lambda e, h=h, kt=kt, sbk=sbk: e.matmul(pb[sbk][:, kt * W:(kt + 1) * W], K_all[:, h * LT + kt * 128:h * LT + kt * 128 + 128],
                                                                             bt(B_Q + h, 0, W), start=True, stop=True),
                                  kdeps + [tB[B_Q + h]], [tPb[sbk]])
                        if kt0 > 0:
                            P.add("act", lambda e, sbk=sbk, pi=pi: e.activation(out=bt(pi, 0, kt0 * W), in_=pb[sbk][:, 0:kt0 * W], func=AF.Exp, scale=SCALE),
                                  [tPb[sbk]], [tB[pi]])
                        dc = hh * W
                        P.add("pe", lambda e, h=h, dc=dc: e.matmul(pb[5][0:W, dc:dc + W], K_all[:, h * LT + t0:h * LT + t0 + W],
                                                                 bt(B_Q + h, 0, W), start=True, stop=False),
                              kdeps + [tB[B_Q + h]], [tPb[5]])
                        P.add("pe", lambda e, dc=dc: e.matmul(pb[5][0:W, dc:dc + W], cbf[0:W, 0:W], cbf[0:W, 256:256 + W], start=False, stop=True),
                              [tConst], [tPb[5]])
                        P.add("act", lambda e, dc=dc: e.activation(out=bt(B_G + 3, dc, dc + W, 0, W), in_=pb[5][0:W, dc:dc + W], func=AF.Exp, scale=SCALE),
                              [tPb[5]], [tB[B_G + 3]])
                    for hh in range(2):
                        h = 2 * p + hh
                        ob = 6 + hh
                        pi = B_G + 1 + hh
                        dc = hh * W
                        vs = h * 65 if hh == 0 else (h - 1) * 65 + 1
                        vdeps = [t for gg in range(g + 1) for t in tV[gg]]
                        for kt in range(kt0):
                            P.add("pe", lambda e, ob=ob, pi=pi, kt=kt, vs=vs: e.matmul(pb[ob][:, 0:W], V_all[:, kt * 520 + vs:kt * 520 + vs + 128],
                                                                                    bt(pi, kt * W, (kt + 1) * W), start=(kt == 0), stop=False),
                                  vdeps + [tB[pi]], [tPb[ob]])
                        P.add("pe", lambda e, ob=ob, dc=dc, vs=vs: e.matmul(pb[ob][:, 0:W], V_all[0:W, kt0 * 520 + vs:kt0 * 520 + vs + 128],
                                                                         bt(B_G + 3, dc, dc + W, 0, W), start=(kt0 == 0), stop=True),
                              vdeps + [tB[B_G + 3]], [tPb[ob]])
                else:
                    qk(0)
                    if n_it > 1:
                        qk(1)
                    if pending[0] is not None:
                        norm_tail(pending[0])
                        pending[0] = None
                    for i in range(n_it):
                        if i + 2 < n_it:
                            qk(i + 2)
                        pv(i)
                P.add("dve", lambda e: e.tensor_copy(out=ft(f_x, 0, W, 0, 64), in_=pb[7][0:64, 0:W]), [tPb[7]], [tF[f_x]])
                P.add("act", lambda e: e.copy(out=ft(f_x, 0, W, 64, 128), in_=pb[6][64:128, 0:W]), [tPb[6]], [tF[f_x]])
                P.add("dve", lambda e: e.tensor_copy(out=ft(f_u, 0, W, 0, 64), in_=pb[6][0:64, 0:W]), [tPb[6]], [tF[f_u]])
                P.add("act", lambda e: e.copy(out=ft(f_u, 0, W, 64, 128), in_=pb[7][64:128, 0:W]), [tPb[7]], [tF[f_u]])
                pending[0] = p
            norm_tail(pending[0])
            gstate["att"] = False
            cnt["gb"] = 0
            kch = [(0, 2), (2, 6), (6, 8), (8, 10)]
            for c in range(8):
                s_op = ring_load_unit(l, U_OP + c)
                f_acc = F_X + c % 2
                f_tmp = F_X + 2 + c % 2
                for i in range(4):
                    b = zmm(l, VC_GL + c * 4 + i, W)
                    gi = B_G + i
                    P.add("act", lambda e, b=b, gi=gi, c=c, i=i: e.activation(out=bt(gi, 0, W), in_=pb[b][:, 0:W], func=AF.Sigmoid,
                                                                              bias=col(l, C_GBIAS + c * 4 + i), scale=1.0), [tPb[b], tConst], [tB[gi]])
                    by = gbank()
                    k0, k1 = kch[i]
                    for k in range(k0, k1):
                        P.add("pe", lambda e, by=by, k=k, k0=k0, k1=k1, s_op=s_op: e.matmul(pb[by][:, 0:W], uring[:, s_op * UW + k * 128:s_op * UW + (k + 1) * 128],
                                                                                 bt(B_U + k, 0, W), start=(k == k0), stop=(k == k1 - 1)),
                              [tUR[s_op], tB[B_U + k]], [tPb[by]])
                    if i == 0:
                        P.add("dve", lambda e, by=by, gi=gi, f_acc=f_acc: e.tensor_tensor(out=ft(f_acc, 0, W), in0=pb[by][:, 0:W], in1=bt(gi, 0, W), op=ALU.mult),
                              [tPb[by], tB[gi]], [tF[f_acc]])
                    else:
                        P.add("dve", lambda e, by=by, gi=gi, f_tmp=f_tmp: e.tensor_tensor(out=ft(f_tmp, 0, W), in0=pb[by][:, 0:W], in1=bt(gi, 0, W), op=ALU.mult),
                              [tPb[by], tB[gi]], [tF[f_tmp]])
                        if i < 3:
                            P.add("pool", lambda e, f_acc=f_acc, f_tmp=f_tmp: e.tensor_tensor(out=ft(f_acc, 0, W), in0=ft(f_acc, 0, W), in1=ft(f_tmp, 0, W), op=ALU.add),
                                  [tF[f_acc], tF[f_tmp]], [tF[f_acc]])
                        else:
                            P.add("pool", lambda e, c=c, f_acc=f_acc, f_tmp=f_tmp: e.tensor_tensor(out=bt(B_Q + c, 0, W), in0=ft(f_acc, 0, W), in1=ft(f_tmp, 0, W), op=ALU.add),
                                  [tF[f_acc], tF[f_tmp]], [tB[B_Q + c]])
            if debug:
                for i in range(10):
                    P.add("sp", lambda e, i=i: e.dma_start(out=dbg_d[g, :, i * 512:i * 512 + W], in_=bt(B_U + i, 0, W)),
                          [tB[B_U + i]], [], lane=tB[B_U + i].lane)
                for i in range(8):
                    P.add("sp", lambda e, i=i: e.dma_start(out=dbg_d[g, :, (10 + i) * 512:(10 + i) * 512 + W], in_=bt(B_Q + i, 0, W)),
                          [tB[B_Q + i]], [], lane=tB[B_Q + i].lane)
            cnt["x"] = 0
            if g + 1 < NG:
                make_hT(l, g + 1)
            sq_prev = None
            for c in range(8):
                s_wo = ring_load_unit(l, U_WO + c)
                b = gbank()
                for k in range(8):
                    P.add("pe", lambda e, b=b, k=k, s_wo=s_wo: e.matmul(pb[b][:, 0:W], uring[:, s_wo * UW + k * 128:s_wo * UW + (k + 1) * 128],
                                                             bt(B_Q + k, 0, W), start=(k == 0), stop=(k == 7)),
                          [tUR[s_wo], tB[B_Q + k]], [tPb[b]])
                if sq_prev is not None:
                    P.add("pe", lambda e, sq=sq_prev, c=c: e.matmul(pb[ST][:, 0:W], ones, bt(sq, 0, W), start=(c == 1), stop=False),
                          [tB[sq_prev], tConst], [tPb[ST]])
                sq = rot("sq", B_SQ, 3)
                P.add("act", lambda e, b=b, sq=sq: e.activation(out=bt(sq, 0, W), in_=pb[b][:, 0:W], func=AF.Square), [tPb[b]], [tB[sq]])
                P.add("act", lambda e, b=b, c=c: e.copy(out=ft(F_O + c, 0, W), in_=pb[b][:, 0:W]), [tPb[b]], [tF[F_O + c]])
                sq_prev = sq
            P.add("pe", lambda e, sq=sq_prev: e.matmul(pb[ST][:, 0:W], ones, bt(sq, 0, W), start=False, stop=True),
                  [tB[sq_prev], tConst], [tPb[ST]])
            f_rp = F_X + 3
            rstd_from_stat(ST, W, 1.0 / D, f_rp)
            pending_final[0] = (l, g)
            cnt["x"] = 0

        for l in range(n_layers):
            P.add("pool", lambda e: e.memset(pvbuf[:], 0.0), [], [tPv])
            P.add("pool", lambda e: e.memset(glu[:], 0.0), [], [tGlu])
            P.add("pool", lambda e: e.memset(prodb[:], 0.0), [], [tProd])
            if l + 1 < n_layers:
                cast_q.extend([(l + 1, "i", v) for v in range(NVC)] + [(l + 1, "u", u) for u in range(NU)])
            prenorm1(l, 0)
            cnt["x"] = 0
            make_hT(l, 0)
            cnt["x"] = 0
            for g in range(NG):
                layer_group(l, g)
            while cast_q:
                emit_cast_chunk(*cast_q.pop(0))
            if pending_final[0] is not None:
                emit_final(*pending_final[0])
                pending_final[0] = None

        def semctx(name):
            return es.enter_context(nc.semaphore(name))

        P.finalize(semctx)
        with nc.Block() as block:
            @block.sync
            def _(e):
                P.emit("sp", e)

            @block.tensor
            def _(e):
                P.emit("pe", e)

            @block.scalar
            def _(e):
                P.emit("act", e)

            @block.vector
            def _(e):
                P.emit("dve", e)

            @block.gpsimd
            def _(e):
                P.emit("pool", e)
    return nc


def _prep_weights(inp):
    f32 = np.float32
    w_in = np.asarray(inp["w_in"], f32)
    win = np.zeros((NL, NVC, 128, 8, 128), f32)

    def put(l, vc, cols_src, dst0=0):
        blk = w_in[l][:, cols_src]
        n = blk.shape[1]
        win[l, vc, :, :, dst0:dst0 + n] = blk.reshape(8, 128, n).transpose(1, 0, 2)

    perm = np.concatenate([np.arange(16, 32), np.arange(0, 16)])
    for l in range(NL):
        for k in range(2):
            put(l, VC_PV + k, np.arange(0 + 128 * k, 128 * (k + 1)))
            put(l, VC_PG + k, np.arange(256 + 128 * k, 256 + 128 * (k + 1)))
            put(l, VC_CQ + k, np.arange(512 + 128 * k, 512 + 128 * (k + 1)))
            put(l, VC_CA + k, np.arange(1440 + 128 * k, 1440 + 128 * (k + 1)))
            put(l, VC_CGATE + k, np.arange(1696 + 128 * k, 1696 + 128 * (k + 1)))
            put(l, VC_CG + k, np.arange(1952 + 128 * k, 1952 + 128 * (k + 1)))
            put(l, VC_BG + k, np.arange(2208 + 128 * k, 2208 + 128 * (k + 1)))
            put(l, VC_CGS + k, np.arange(2464 + 128 * k, 2464 + 128 * (k + 1)))
            put(l, VC_XV + k, np.arange(2720 + 128 * k, 2720 + 128 * (k + 1)))
            put(l, VC_SG + k, np.arange(2976 + 128 * k, 2976 + 128 * (k + 1)))
        put(l, VC_CKV, np.arange(768, 896))
        put(l, VC_KRA, np.arange(896, 928), 64)
        put(l, VC_KRA, 896 + perm, 96)
        put(l, VC_KRB, 896 + perm, 64)
        put(l, VC_KRB, np.arange(896, 928), 96)
        for p in range(4):
            put(l, VC_MG + p, np.arange(928 + 128 * p, 928 + 128 * (p + 1)))
        for c in range(8):
            for i in range(4):
                put(l, VC_GL + c * 4 + i, np.arange(3232 + 1024 * i + 128 * c, 3232 + 1024 * i + 128 * (c + 1)))
    win_f = np.ascontiguousarray(win.transpose(0, 2, 1, 3, 4).reshape(NL * 128, NVC * 1024))

    wun = np.zeros((NL, 128, NU, UW), f32)
    pool_w = np.asarray(inp["pool_w"], f32)
    w_uq = np.asarray(inp["w_uq"], f32)
    w_ukv = np.asarray(inp["w_ukv"], f32)
    cf_w = np.asarray(inp["conf_dw_w"], f32)
    sc_w = np.asarray(inp["sc_dw_w"], f32)
    w_outs = [np.asarray(inp[k], f32) for k in ("w_out_pool", "w_out_mla", "w_out_conf", "w_out_sc")]
    w_o = np.asarray(inp["w_o"], f32)
    ar = np.arange(128)
    for l in range(NL):
        kv = w_ukv[l].reshape(128, 8, 128)
        wun[l, :, U_UKV, 0:512] = kv[:, :, 0:64].reshape(128, 512)
        wun[l, :, U_UKV, 512:1024] = kv[:, :, 64:128].reshape(128, 512)
        for k in range(2):
            bd = np.zeros((128, 128), f32)
            bd[0:64, 0:64] = pool_w[l, 2 * k]
            bd[64:128, 64:128] = pool_w[l, 2 * k + 1]
            wun[l, :, U_UKV, 1024 + 128 * k:1152 + 128 * k] = bd
        for h in range(8):
            qa = w_uq[l][:, 96 * h:96 * h + 96]
            qx = np.concatenate([qa, qa[:, 64 + perm]], axis=1)
            u, o_ = U_UQ + h // 3, (h % 3) * 256
            for k in range(2):
                wun[l, :, u, o_ + k * 128:o_ + (k + 1) * 128] = qx[128 * k:128 * (k + 1)]
        for k in range(2):
            for j in range(31):
                idx = k * 31 + j
                dm = np.zeros((128, 128), f32)
                dm[ar, ar] = cf_w[l, j, 128 * k:128 * (k + 1)]
                wun[l, :, U_CF + idx // 10, (idx % 10) * 128:(idx % 10 + 1) * 128] = dm
            for j in range(3):
                idx = k * 3 + j
                dm = np.zeros((128, 128), f32)
                dm[ar, ar] = sc_w[l, j, 128 * k:128 * (k + 1)]
                wun[l, :, U_SC, idx * 128:(idx + 1) * 128] = dm
        wcat = np.concatenate([w[l] for w in w_outs], axis=0)
        for c in range(8):
            blk = wcat[:, 128 * c:128 * (c + 1)].reshape(10, 128, 128).transpose(1, 0, 2).reshape(128, 1280)
            wun[l, :, U_OP + c, :] = blk
            blk = w_o[l][:, 128 * c:128 * (c + 1)].reshape(8, 128, 128).transpose(1, 0, 2).reshape(128, 1024)
            wun[l, :, U_WO + c, 0:1024] = blk
    wun_f = np.ascontiguousarray(wun.reshape(NL * 128, NU * UW))

    cols = np.zeros((128, NL, NCOLL), f32)
    for l in range(NL):
        cols[:, l, C_GPRE:C_GPRE + 8] = np.asarray(inp["pre_norm_g"], f32)[l].reshape(8, 128).T
        cols[:, l, C_GPOST:C_GPOST + 8] = np.asarray(inp["post_norm_g"], f32)[l].reshape(8, 128).T
        gb = np.asarray(inp["gate_bias"], f32)[l].reshape(4, 8, 128)
        cols[:, l, C_GBIAS:C_GBIAS + 32] = gb.transpose(2, 1, 0).reshape(128, 32)
        cols[:, l, C_PSC:C_PSC + 2] = np.asarray(inp["pool_scale"], f32)[l].reshape(2, 128).T
        cols[:, l, C_QG:C_QG + 2] = np.asarray(inp["q_norm_g"], f32)[l].reshape(2, 128).T
        cols[:, l, C_KVG] = np.asarray(inp["kv_norm_g"], f32)[l]
        cols[:, l, C_CFB:C_CFB + 2] = np.asarray(inp["conf_dw_b"], f32)[l].reshape(2, 128).T
        cols[:, l, C_LNG:C_LNG + 2] = np.asarray(inp["conf_ln_g"], f32)[l].reshape(2, 128).T
        cols[:, l, C_LNB:C_LNB + 2] = np.asarray(inp["conf_ln_b"], f32)[l].reshape(2, 128).T
    cols = np.ascontiguousarray(cols.reshape(128, NL * NCOLL))
    return win_f, wun_f, cols


def _consts():
    f32 = np.float32
    cbf = np.zeros((128, 384), f32)
    cbf[:, 0:128] = np.eye(128, dtype=f32)
    cbf[:, 128:256] = 1.0
    k = np.arange(128)[:, None]
    q = np.arange(128)[None, :]
    cbf[:, 256:384] = np.where(k > q, -30000.0, 0.0)
    cf32 = np.zeros((128, 160), f32)
    cf32[64, 0:64] = 1.0
    cf32[63, 64:128] = 1.0
    wins = {0: (2, 4), 1: (8, 16)}
    t = np.arange(16)
    for kk in range(2):
        for half in range(2):
            w = wins[kk][half]
            cf32[64 * half:64 * half + 64, 128 + 16 * kk:144 + 16 * kk] = 1.0 / np.minimum(t + 1, w).astype(f32)
    inv = 1.0 / (10000.0 ** (np.arange(0, 32, 2, dtype=f32) / 32))
    ang = np.arange(LT, dtype=f32)[:, None] * inv[None, :]
    cos = np.cos(ang).astype(f32).T
    sin = np.sin(ang).astype(f32).T
    cs = np.zeros((128, 2, LT), f32)
    sgn = np.concatenate([-sin, sin], axis=0)
    cc = np.concatenate([cos, cos], axis=0)
    cs[0:64, 0] = 1.0
    cs[64:96, 0] = cc
    cs[96:128, 0] = sgn
    cs[64:96, 1] = sgn
    cs[96:128, 1] = cc
    return cbf, cf32, np.ascontiguousarray(cs.reshape(128, 2 * LT))


_CACHE = {}


def kernel(**inp):
    x = np.asarray(inp["x"], np.float32)
    meta = np.asarray(inp["meta_tokens"], np.float32)
    B = x.shape[0]
    win_f, wun_f, cols = _prep_weights(inp)
    cbf, cf32, cs = _consts()
    if "nc" not in _CACHE:
        _CACHE["nc"] = build_nc()
    nc = _CACHE["nc"]
    in_maps = []
    for b in range(B):
        xin = np.ascontiguousarray(np.concatenate([meta, x[b]], axis=0).T)
        in_maps.append({"xin": xin, "win_f": win_f, "wun_f": wun_f, "cols": cols, "cbf": cbf, "cf32": cf32, "cs": cs})
    res = run_bass_kernel_spmd(nc, in_maps, core_ids=list(range(B)))
    out = np.empty((B, SEQ, D), np.float32)
    for b in range(B):
        out[b] = res.results[b]["y"][:, NMETA:].T
    return out
```

```python
import numpy as np
from contextlib import ExitStack
import concourse.bass as bass
import concourse.mybir as mybir
from concourse.bass_utils import run_bass_kernel_spmd

F32 = mybir.dt.float32
BF16 = mybir.dt.bfloat16
AF = mybir.ActivationFunctionType
ALU = mybir.AluOpType

D = 1024
NL = 4
SEQ = 4096
NMETA = 16
LT = SEQ + NMETA
EPS = 1e-6
NVC = 59
NU = 28
UW = 1280
NCOLL = 59
SCALE = 96 ** -0.5

U_UKV = 0
U_UQ = 1
U_CF = 4
U_SC = 11
U_OP = 12
U_WO = 20
C_GPRE, C_GPOST, C_GBIAS, C_PSC, C_QG, C_KVG, C_CFB, C_LNG, C_LNB = 0, 8, 16, 48, 50, 52, 53, 55, 57

VC_PV, VC_PG, VC_CQ, VC_CKV, VC_KRA, VC_KRB, VC_MG = 0, 2, 4, 6, 7, 8, 9
VC_CA, VC_CGATE, VC_CG, VC_BG, VC_CGS, VC_XV, VC_SG, VC_GL = 13, 15, 17, 19, 21, 23, 25, 27


class Lane:
    def __init__(s, name, waitall=False):
        s.name = name
        s.count = 0
        s.sem = None
        s.waitall = waitall


class T:
    __slots__ = ("name", "w", "r", "lane")

    def __init__(s, name="", lane=None):
        s.name = name
        s.w = None
        s.r = {}
        s.lane = lane


class Op:
    __slots__ = ("eng", "idx", "fn", "deps", "signal", "seq", "lane", "lseq", "waits")


COMPUTE = ("pe", "act", "dve", "pool")
ENGS = ("pe", "act", "dve", "pool", "sp")


class Prog:
    def __init__(s):
        s.ops = {e: [] for e in ENGS}
        s.lanes = []

    def lane(s, name, waitall=False):
        l = Lane(name, waitall)
        s.lanes.append(l)
        return l

    def add(s, eng, fn, reads=(), writes=(), lane=None):
        o = Op()
        o.eng = eng
        o.idx = len(s.ops[eng])
        o.fn = fn
        o.signal = False
        o.seq = 0
        o.lane = lane
        o.waits = None
        if lane is not None:
            lane.count += 1
            o.lseq = lane.count
        else:
            o.lseq = 0
        deps = []
        for t in reads:
            if t.w is not None:
                deps.append((t.w, True))
        for t in writes:
            if t.w is not None:
                deps.append((t.w, False))
            for d in t.r.values():
                deps.append((d, False))
        key = lane if lane is not None else eng
        for t in reads:
            t.r[key] = o
        for t in writes:
            t.w = o
            t.r = {}
        o.deps = deps
        s.ops[eng].append(o)
        return o

    def finalize(s, semctx):
        for eng in ENGS:
            waited = {}
            for o in s.ops[eng]:
                need = {}
                for d, raw in o.deps:
                    if d is o:
                        continue
                    if d.lane is not None:
                        if d.lane.waitall and d.lane is o.lane:
                            continue
                        k = d.lane
                        v = d.lseq
                    else:
                        if d.eng == eng and o.lane is None:
                            if eng == "pe" or not raw:
                                continue
                        k = d.eng
                        v = d.idx
                    if waited.get(k, -1) >= v:
                        continue
                    if k not in need or need[k][0] < v:
                        need[k] = (v, d)
                o.waits = []
                for k, (v, d) in need.items():
                    waited[k] = v
                    if d.lane is None:
                        d.signal = True
                    o.waits.append(d)
                o.deps = None
        for eng in COMPUTE:
            c = 0
            for o in s.ops[eng]:
                if o.signal:
                    c += 1
                    o.seq = c
        s.sems = {}
        for eng in COMPUTE:
            s.sems[eng] = semctx("e_" + eng)
        for l in s.lanes:
            if l.count > 0:
                l.sem = semctx("l_" + l.name)

    def emit(s, eng, e):
        for o in s.ops[eng]:
            for d in o.waits:
                if d.lane is not None:
                    v = 16 * (d.lane.count if d.lane.waitall else d.lseq)
                    e.wait_ge(d.lane.sem, v)
                else:
                    e.wait_ge(s.sems[d.eng], d.seq)
            ins = o.fn(e)
            if o.lane is not None:
                ins.then_inc(o.lane.sem, 16)
            elif o.signal:
                ins.then_inc(s.sems[eng], 1)
        if eng == "sp":
            for l in s.lanes:
                if l.count > 0:
                    e.wait_ge(l.sem, 16 * l.count)


def build_nc(n_layers=NL, debug=False, groups=None):
    nc = bass.Bass("TRN2", target_bir_lowering=False)
    P = Prog()
    if groups is None:
        groups = [(g * 512, 512) for g in range(8)] + [(4096, 16)]
    NG = len(groups)

    xin_d = nc.dram_tensor("xin", [D, LT], F32, kind="ExternalInput").ap()
    win_f = nc.dram_tensor("win_f", [NL * 128, NVC * 1024], F32, kind="ExternalInput").ap()
    wun_f = nc.dram_tensor("wun_f", [NL * 128, NU * UW], F32, kind="ExternalInput").ap()
    cols_d = nc.dram_tensor("cols", [128, NL * NCOLL], F32, kind="ExternalInput").ap()
    cbf_d = nc.dram_tensor("cbf", [128, 384], F32, kind="ExternalInput").ap()
    cf32_d = nc.dram_tensor("cf32", [128, 160], F32, kind="ExternalInput").ap()
    cs_d = nc.dram_tensor("cs", [128, 2 * LT], F32, kind="ExternalInput").ap()
    y_d = nc.dram_tensor("y", [D, LT], F32, kind="ExternalOutput").ap()
    win_b = nc.dram_tensor("win_b", [NL * 128, NVC * 1024], BF16, kind="Internal").ap()
    wun_b = nc.dram_tensor("wun_b", [NL * 128, NU * UW], BF16, kind="Internal").ap()
    dbg_d = None
    if debug:
        dbg_d = nc.dram_tensor("dbg", [NG, 128, 18 * 512], BF16, kind="ExternalOutput").ap()

    with ExitStack() as es:
        def sb(name, shape, dt):
            return es.enter_context(nc.sbuf_tensor(name, shape, dt))

        K_all = sb("K_all", [128, 8 * LT], BF16)
        V_all = sb("V_all", [128, 33 * 520], BF16)
        NVS = 5
        NUS = 3
        vring = sb("vring", [128, NVS * 1024], BF16)
        uring = sb("uring", [128, NUS * UW], BF16)
        hT = sb("hT", [128, 8 * 512], BF16)
        NF = 14
        NB = 30
        ftm = sb("ftm", [128, NF * 512], F32)
        btm = sb("btm", [128, NB * 512], BF16)
        pvbuf = sb("pvbuf", [128, 2 * 528], F32)
        pta = sb("pta", [128, 528], F32)
        ptb = sb("ptb", [128, 528], F32)
        glu = sb("glu", [128, 2 * 544], BF16)
        prodb = sb("prodb", [128, 2 * 520], BF16)
        cs = sb("cs_sb", [128, 1024], F32)
        cols = sb("cols_sb", [128, NL * NCOLL], F32)
        cbf_f = sb("cbf_f", [128, 384], F32)
        cbf = sb("cbf_b", [128, 384], BF16)
        cf32 = sb("cf32_sb", [128, 160], F32)
        pb = [es.enter_context(nc.psum_tensor(f"pb{i}", [128, 512], F32)) for i in range(8)]

        ident = cbf[:, 0:128]
        ones = cbf[:, 128:256]
        sel = cf32[:, 0:128]

        tKh = [[T(f"K{g}_{h}") for h in range(8)] for g in range(NG)]
        tKr = [T(f"Kr{g}") for g in range(NG)]
        tV = [[T(f"V{g}_{j}") for j in range(4)] for g in range(NG)]
        tVR = [T(f"vring{i}", P.lane(f"vring{i}")) for i in range(NVS)]
        tUR = [T(f"uring{i}", P.lane(f"uring{i}")) for i in range(NUS)]
        tH = T("hT")
        tF = [T(f"ft{i}", P.lane(f"ft{i}")) for i in range(NF)]
        tB = [T(f"bt{i}", P.lane(f"bt{i}")) for i in range(NB)]
        tPv, tPta, tPtb, tGlu, tProd = T("pv"), T("pta"), T("ptb"), T("glu"), T("prod")
        tCs = T("cs", P.lane("cs"))
        tConst = T("const")
        tPb = [T(f"pb{i}") for i in range(8)]
        tX = [[T(f"X{g}_{c}") for c in range(8)] for g in range(NG)]
        Linit = P.lane("init", waitall=True)
        Ldbg = P.lane("dbg", waitall=True)

        def ft(i, a=0, b=512, p0=0, p1=128):
            return ftm[p0:p1, i * 512 + a:i * 512 + b]

        def bt(i, a=0, b=512, p0=0, p1=128):
            return btm[p0:p1, i * 512 + a:i * 512 + b]

        btm_f = btm.bitcast(F32)

        def fa(j, a=0, b=512, p0=0, p1=128):
            return btm_f[p0:p1, j * 512 + a:j * 512 + b]

        def tA(j):
            return [tB[2 * j], tB[2 * j + 1]]

        def col(l, c):
            return cols[:, l * NCOLL + c:l * NCOLL + c + 1]

        B_Q = 0
        B_U = 8
        B_G = 18
        B_SQ = 22
        B_CQN = 25
        B_CKVN = 27
        B_P = 28
        F_O = 0
        F_RPRE = 8
        F_X = 10

        cnt = {"gb": 0, "ring": 0, "uring": 0, "z": 0, "g": 0, "sq": 0, "x": 0, "p": 0, "f": 0, "s": 0}

        ST = 2
        GB_ALL = [0, 1, 3, 4, 5, 6, 7]
        GB_ATT = [0, 1]
        gstate = {"att": False}

        def gbank():
            lst = GB_ATT if gstate["att"] else GB_ALL
            i = lst[cnt["gb"] % len(lst)]
            cnt["gb"] += 1
            return i

        def rot(key, base, n):
            i = base + cnt[key] % n
            cnt[key] += 1
            return i

        P.add("sp", lambda e: e.dma_start(out=cols[:], in_=cols_d), [], [tConst], lane=Linit)
        P.add("sp", lambda e: e.dma_start(out=cbf_f[:], in_=cbf_d), [], [tConst], lane=Linit)
        P.add("sp", lambda e: e.dma_start(out=cf32[:], in_=cf32_d), [], [tConst], lane=Linit)
        P.add("dve", lambda e: e.tensor_copy(out=cbf[:], in_=cbf_f[:]), [tConst], [tConst])
        P.add("pool", lambda e: e.memset(V_all[:], 1.0), [], [t for tv in tV for t in tv])
        for g_, (t0_, W_) in enumerate(groups):
            lx = P.lane(f"xcp{g_}")
            P.add("sp", lambda e, t0_=t0_, W_=W_: e.dma_start(out=y_d[:, t0_:t0_ + W_], in_=xin_d[:, t0_:t0_ + W_]),
                  [], tX[g_], lane=lx)
        NWP = 15
        NUP = 4
        tWinP = [T(f"winp{j}") for j in range(NWP)]
        tWunP = [T(f"wunp{j}") for j in range(NUP)]
        tWinC = [[T(f"winc{l}_{v}") for v in range(NVC)] for l in range(NL)]
        tWunC = [[T(f"wunc{l}_{u}") for u in range(NU)] for l in range(NL)]
        tChain = T("castchain")
        Lchain = P.lane("castchain")
        cast_q = []
        LWinP = [P.lane(f"cwi{j}") for j in range(NWP)]
        LWunP = [P.lane(f"cwu{j}") for j in range(NUP)]
        cast_order = [("i", 1), ("i", 2), ("i", 0), ("u", 0), ("i", 6), ("i", 5), ("i", 4), ("i", 3), ("u", 1), ("u", 2),
                      ("i", 7), ("i", 8), ("i", 9), ("u", 3), ("i", 10), ("i", 11), ("i", 12), ("i", 13), ("i", 14)]

        def emit_cast(l, kind, j):
            if kind == "i":
                c0_ = j * 4096
                c1_ = min((j + 1) * 4096, NVC * 1024)
                P.add("pool", lambda e: e.dma_start(out=win_b[l * 128:(l + 1) * 128, c0_:c1_], in_=win_f[l * 128:(l + 1) * 128, c0_:c1_]),
                      [], [tWinP[j]], lane=LWinP[j])
            else:
                c0_ = j * 7 * UW
                c1_ = (j + 1) * 7 * UW
                P.add("pool", lambda e: e.dma_start(out=wun_b[l * 128:(l + 1) * 128, c0_:c1_], in_=wun_f[l * 128:(l + 1) * 128, c0_:c1_]),
                      [], [tWunP[j]], lane=LWunP[j])

        def emit_cast_chunk(l, kind, i):
            if kind == "i":
                P.add("pool", lambda e: e.dma_start(out=win_b[l * 128:(l + 1) * 128, i * 1024:(i + 1) * 1024],
                                                    in_=win_f[l * 128:(l + 1) * 128, i * 1024:(i + 1) * 1024]),
                      [tChain], [tChain, tWinC[l][i]], lane=Lchain)
            else:
                P.add("pool", lambda e: e.dma_start(out=wun_b[l * 128:(l + 1) * 128, i * UW:(i + 1) * UW],
                                                    in_=wun_f[l * 128:(l + 1) * 128, i * UW:(i + 1) * UW]),
                      [tChain], [tChain, tWunC[l][i]], lane=Lchain)

        for (kind, j) in cast_order:
            emit_cast(0, kind, j)

        def ring_load_vc(l, vc):
            s_ = cnt["ring"] % NVS
            cnt["ring"] += 1
            P.add("sp", lambda e: e.dma_start(out=vring[:, s_ * 1024:(s_ + 1) * 1024],
                                              in_=win_b[l * 128:(l + 1) * 128, vc * 1024:(vc + 1) * 1024]),
                  [tWinP[vc // 4] if l == 0 else tWinC[l][vc]], [tVR[s_]], lane=tVR[s_].lane)
            return s_

        def ring_load_unit(l, u):
            s_ = cnt["uring"] % NUS
            cnt["uring"] += 1
            uw_ = 1024 if u >= U_WO else UW
            P.add("sp", lambda e: e.dma_start(out=uring[:, s_ * UW:s_ * UW + uw_],
                                              in_=wun_b[l * 128:(l + 1) * 128, u * UW:u * UW + uw_]),
                  [tWunP[u // 7] if l == 0 else tWunC[l][u]], [tUR[s_]], lane=tUR[s_].lane)
            return s_

        def zmm(l, vc, W):
            cnt["z"] += 1
            if cast_q and cnt["z"] % 5 == 0:
                emit_cast_chunk(*cast_q.pop(0))
            s_ = ring_load_vc(l, vc)
            b = gbank()
            for dc in range(8):
                P.add("pe", lambda e, dc=dc: e.matmul(pb[b][:, 0:W], vring[:, s_ * 1024 + dc * 128:s_ * 1024 + (dc + 1) * 128],
                                                      hT[:, dc * 512:dc * 512 + W], start=(dc == 0), stop=(dc == 7)),
                      [tVR[s_], tH], [tPb[b]])
            return b

        def x_src(l):
            return y_d

        LpoolF = {i: P.lane(f"ftp{i}") for i in range(NF)}

        def load_x(l, g, c, fi, eng="pool"):
            t0, W = groups[g]
            src = x_src(l)
            ln = tF[fi].lane if eng == "sp" else LpoolF[fi]
            P.add(eng, lambda e: e.dma_start(out=ft(fi, 0, W), in_=src[c * 128:(c + 1) * 128, t0:t0 + W]),
                  [tX[g][c]], [tF[fi]], lane=ln)

        def rstd_from_stat(bank, W, scale, fo):
            P.add("act", lambda e: e.activation(out=ft(fo, 0, W), in_=pb[bank][:, 0:W], func=AF.Ln, bias=EPS, scale=scale),
                  [tPb[bank]], [tF[fo]])
            P.add("act", lambda e: e.activation(out=ft(fo, 0, W), in_=ft(fo, 0, W), func=AF.Exp, scale=-0.5), [tF[fo]], [tF[fo]])

        def prenorm1(l, g):
            t0, W = groups[g]
            fr = F_RPRE + g % 2
            for c in range(8):
                fi = rot("x", F_X, 4)
                load_x(l, g, c, fi)
                bi = rot("sq", B_SQ, 3)
                P.add("act", lambda e, fi=fi, bi=bi: e.activation(out=bt(bi, 0, W), in_=ft(fi, 0, W), func=AF.Square),
                      [tF[fi]], [tB[bi]])
                P.add("pe", lambda e, bi=bi, c=c: e.matmul(pb[ST][:, 0:W], ones, bt(bi, 0, W), start=(c == 0), stop=(c == 7)),
                      [tB[bi], tConst], [tPb[ST]])
            rstd_from_stat(ST, W, 1.0 / D, fr)

        def make_hT(l, g):
            t0, W = groups[g]
            fr = F_RPRE + g % 2
            for c in range(8):
                fi = rot("x", F_X, 4)
                load_x(l, g, c, fi)
                P.add("dve", lambda e, fi=fi, c=c: e.scalar_tensor_tensor(
                    out=hT[:, c * 512:c * 512 + W], in0=ft(fi, 0, W), scalar=col(l, C_GPRE + c), in1=ft(fr, 0, W),
                    op0=ALU.mult, op1=ALU.mult), [tF[fi], tF[fr], tConst], [tH])

        pending_final = [None]
        LKr = P.lane("krope")

        def emit_final(l, g, chunks=range(8)):
            t0, W = groups[g]
            f_rp = F_X + 3
            for c in chunks:
                P.add("dve", lambda e, c=c: e.scalar_tensor_tensor(out=ft(F_O + c, 0, W), in0=ft(F_O + c, 0, W), scalar=col(l, C_GPOST + c),
                                                                   in1=ft(f_rp, 0, W), op0=ALU.mult, op1=ALU.mult),
                      [tF[F_O + c], tF[f_rp], tConst], [tF[F_O + c]])
                P.add("pool", lambda e, c=c: e.dma_start(out=y_d[c * 128:(c + 1) * 128, t0:t0 + W], in_=ft(F_O + c, 0, W), accum_op=ALU.add),
                      [tF[F_O + c]], [tX[g][c]], lane=LpoolF[F_O + c])

        def layer_group(l, g):
            t0, W = groups[g]
            n4 = (W + 127) // 128
            kt0 = t0 // 128
            P.add("sp", lambda e: e.dma_start(out=cs[:, 0:W], in_=cs_d[:, t0:t0 + W]), [], [tCs], lane=tCs.lane)
            P.add("sp", lambda e: e.dma_start(out=cs[:, 512:512 + W], in_=cs_d[:, LT + t0:LT + t0 + W]), [], [tCs], lane=tCs.lane)
            a_cq = [4, 5]
            a_rq, a_ckv, a_rkv, a_ka, a_kb = 6, 7, 8, 9, 10
            sq_cq = []
            for k in range(2):
                b = zmm(l, VC_CQ + k, W)
                sq = rot("sq", B_SQ, 3)
                P.add("act", lambda e, b=b, sq=sq: e.activation(out=bt(sq, 0, W), in_=pb[b][:, 0:W], func=AF.Square), [tPb[b]], [tB[sq]])
                P.add("act", lambda e, b=b, k=k: e.copy(out=fa(a_cq[k], 0, W), in_=pb[b][:, 0:W]), [tPb[b]], tA(a_cq[k]))
                sq_cq.append(sq)
            b = zmm(l, VC_CKV, W)
            sq_kv = rot("sq", B_SQ, 3)
            P.add("act", lambda e, b=b: e.activation(out=bt(sq_kv, 0, W), in_=pb[b][:, 0:W], func=AF.Square), [tPb[b]], [tB[sq_kv]])
            P.add("act", lambda e, b=b: e.copy(out=fa(a_ckv, 0, W), in_=pb[b][:, 0:W]), [tPb[b]], tA(a_ckv))
            ba = zmm(l, VC_KRA, W)
            P.add("dve", lambda e, ba=ba: e.tensor_tensor(out=fa(a_ka, 0, W, 64, 128), in0=pb[ba][64:128, 0:W], in1=cs[64:128, 0:W], op=ALU.mult),
                  [tPb[ba], tCs], tA(a_ka))
            bb = zmm(l, VC_KRB, W)
            P.add("dve", lambda e, bb=bb: e.tensor_tensor(out=fa(a_kb, 0, W, 64, 128), in0=pb[bb][64:128, 0:W], in1=cs[64:128, 512:512 + W], op=ALU.mult),
                  [tPb[bb], tCs], tA(a_kb))
            P.add("pool", lambda e: e.tensor_tensor(out=K_all[64:128, t0:t0 + W], in0=fa(a_ka, 0, W, 64, 128), in1=fa(a_kb, 0, W, 64, 128), op=ALU.add),
                  tA(a_ka) + tA(a_kb), [tKr[g]])
            P.add("pool", lambda e: e.dma_start(out=bass.AP(K_all, 64 * 8 * LT + LT + t0, [[8 * LT, 64], [LT, 7], [1, W]]),
                                              in_=bass.AP(K_all, 64 * 8 * LT + t0, [[8 * LT, 64], [0, 7], [1, W]])),
                  [tKr[g]], [tKr[g]], lane=LKr)
            if pending_final[0] is not None:
                emit_final(pending_final[0][0], pending_final[0][1], [4, 5])
            f_sg = []
            for k in range(2):
                b = zmm(l, VC_CGATE + k, W)
                fi = F_O + 4 + k
                P.add("act", lambda e, b=b, fi=fi: e.activation(out=ft(fi, 0, W), in_=pb[b][:, 0:W], func=AF.Sigmoid),
                      [tPb[b]], [tF[fi]])
                f_sg.append(fi)
            for k in range(2):
                b = zmm(l, VC_CA + k, W)
                P.add("dve", lambda e, b=b, k=k: e.tensor_tensor(out=glu[:, k * 544 + 32:k * 544 + 32 + W], in0=pb[b][:, 0:W],
                                                                 in1=ft(f_sg[k], 0, W), op=ALU.mult),
                      [tPb[b], tF[f_sg[k]]], [tGlu])
            for k in range(2):
                b = zmm(l, VC_PV + k, W)
                P.add("act", lambda e, b=b, k=k: e.copy(out=pvbuf[:, k * 528 + 16:k * 528 + 16 + W], in_=pb[b][:, 0:W]),
                      [tPb[b]], [tPv])
            g_pg = []
            for k in range(2):
                b = zmm(l, VC_PG + k, W)
                gi = B_G + k
                P.add("act", lambda e, b=b, gi=gi: e.activation(out=bt(gi, 0, W), in_=pb[b][:, 0:W], func=AF.Silu),
                      [tPb[b]], [tB[gi]])
                g_pg.append(gi)
            p_tiles = []
            for k in range(2):
                vo = k * 528
                NW = 16 + W
                P.add("pool", lambda e, vo=vo, NW=NW: e.tensor_tensor(out=pta[:, 1:NW], in0=pvbuf[:, vo:vo + NW - 1],
                                                                       in1=pvbuf[:, vo + 1:vo + NW], op=ALU.add),
                      [tPv], [tPta])
                pi = B_P + k
                if k == 0:
                    P.add("pool", lambda e, NW=NW: e.tensor_tensor(out=ptb[64:128, 3:NW], in0=pta[64:128, 1:NW - 2],
                                                                    in1=pta[64:128, 3:NW], op=ALU.add), [tPta], [tPtb])
                    srcs = [(pta, 0, 64, 0.5), (ptb, 64, 128, 0.25)]
                else:
                    P.add("pool", lambda e, NW=NW: e.tensor_tensor(out=ptb[:, 3:NW], in0=pta[:, 1:NW - 2],
                                                                    in1=pta[:, 3:NW], op=ALU.add), [tPta], [tPtb])
                    P.add("pool", lambda e, NW=NW: e.tensor_tensor(out=pta[:, 7:NW], in0=ptb[:, 3:NW - 4],
                                                                    in1=ptb[:, 7:NW], op=ALU.add), [tPtb], [tPta])
                    P.add("pool", lambda e, NW=NW: e.tensor_tensor(out=ptb[64:128, 15:NW], in0=pta[64:128, 7:NW - 8],
                                                                    in1=pta[64:128, 15:NW], op=ALU.add), [tPta], [tPtb])
                    srcs = [(pta, 0, 64, 0.125), (ptb, 64, 128, 0.0625)]
                for (sbuf_, p0, p1, iw) in srcs:
                    P.add("dve", lambda e, sbuf_=sbuf_, p0=p0, p1=p1, iw=iw, vo=vo, pi=pi: e.scalar_tensor_tensor(
                        out=bt(pi, 0, W, p0, p1), in0=sbuf_[p0:p1, 16:16 + W], scalar=iw, in1=pvbuf[p0:p1, vo + 16:vo + 16 + W],
                        op0=ALU.mult, op1=ALU.subtract), [tPta, tPtb, tPv], [tB[pi]])
                    if g == 0:
                        fx = F_X + 3
                        P.add("pool", lambda e, sbuf_=sbuf_, p0=p0, p1=p1, k=k, fx=fx: e.tensor_tensor(
                            out=ft(fx, 0, 16, p0, p1), in0=sbuf_[p0:p1, 16:32], in1=cf32[p0:p1, 128 + 16 * k:144 + 16 * k],
                            op=ALU.mult), [tPta, tPtb, tConst], [tF[fx]])
                        P.add("pool", lambda e, p0=p0, p1=p1, vo=vo, pi=pi, fx=fx: e.tensor_tensor(
                            out=bt(pi, 0, 16, p0, p1), in0=ft(fx, 0, 16, p0, p1), in1=pvbuf[p0:p1, vo + 16:vo + 32],
                            op=ALU.subtract), [tF[fx], tPv], [tB[pi]])
                p_tiles.append(pi)
            if W >= 16 and g + 1 < NG:
                for k in range(2):
                    P.add("pool", lambda e, k=k: e.tensor_copy(out=pvbuf[:, k * 528:k * 528 + 16],
                                                               in_=pvbuf[:, k * 528 + W:k * 528 + W + 16]), [tPv], [tPv])
            if pending_final[0] is not None:
                emit_final(pending_final[0][0], pending_final[0][1], [0, 1, 2, 3])
            g_bs = []
            for k in range(2):
                b = zmm(l, VC_SG + k, W)
                gi = B_G + 2 + k
                P.add("act", lambda e, b=b, gi=gi: e.activation(out=bt(gi, 0, W), in_=pb[b][:, 0:W], func=AF.Silu),
                      [tPb[b]], [tB[gi]])
                g_bs.append(gi)
            f_bs = []
            for k in range(2):
                b = zmm(l, VC_BG + k, W)
                fi = F_O + k
                P.add("dve", lambda e, b=b, fi=fi, k=k: e.tensor_tensor(out=ft(fi, 0, W), in0=pb[b][:, 0:W], in1=bt(g_bs[k], 0, W),
                                                                        op=ALU.mult), [tPb[b], tB[g_bs[k]]], [tF[fi]])
                f_bs.append(fi)
            f_cgs = []
            for k in range(2):
                b = zmm(l, VC_CGS + k, W)
                fi = F_O + 2 + k
                P.add("act", lambda e, b=b, fi=fi: e.copy(out=ft(fi, 0, W), in_=pb[b][:, 0:W]), [tPb[b]], [tF[fi]])
                f_cgs.append(fi)
            for k in range(2):
                b = zmm(l, VC_XV + k, W)
                P.add("dve", lambda e, b=b, k=k: e.tensor_tensor(out=prodb[:, k * 520 + 2:k * 520 + 2 + W], in0=pb[b][:, 0:W],
                                                                 in1=ft(f_cgs[k], 0, W), op=ALU.mult),
                      [tPb[b], tF[f_cgs[k]]], [tProd])
            if pending_final[0] is not None:
                emit_final(pending_final[0][0], pending_final[0][1], [6, 7])
            pending_final[0] = None
            for k in range(2):
                P.add("pe", lambda e, k=k: e.matmul(pb[ST][:, 0:W], ones, bt(sq_cq[k], 0, W), start=(k == 0), stop=(k == 1)),
                      [tB[sq_cq[k]], tConst], [tPb[ST]])
            P.add("act", lambda e: e.activation(out=fa(a_rq, 0, W), in_=pb[ST][:, 0:W], func=AF.Ln, bias=EPS, scale=1.0 / 256), [tPb[ST]], tA(a_rq))
            P.add("pe", lambda e: e.matmul(pb[6][:, 0:W], ones, bt(sq_kv, 0, W), start=True, stop=True), [tB[sq_kv], tConst], [tPb[6]])
            P.add("act", lambda e: e.activation(out=fa(a_rkv, 0, W), in_=pb[6][:, 0:W], func=AF.Ln, bias=EPS, scale=1.0 / 128), [tPb[6]], tA(a_rkv))
            P.add("act", lambda e: e.activation(out=fa(a_rq, 0, W), in_=fa(a_rq, 0, W), func=AF.Exp, scale=-0.5), tA(a_rq), tA(a_rq))
            P.add("act", lambda e: e.activation(out=fa(a_rkv, 0, W), in_=fa(a_rkv, 0, W), func=AF.Exp, scale=-0.5), tA(a_rkv), tA(a_rkv))
            for k in range(2):
                P.add("dve", lambda e, k=k: e.scalar_tensor_tensor(out=bt(B_CQN + k, 0, W), in0=fa(a_cq[k], 0, W), scalar=col(l, C_QG + k),
                                                                   in1=fa(a_rq, 0, W), op0=ALU.mult, op1=ALU.mult),
                      tA(a_cq[k]) + tA(a_rq) + [tConst], [tB[B_CQN + k]])
            P.add("dve", lambda e: e.scalar_tensor_tensor(out=bt(B_CKVN, 0, W), in0=fa(a_ckv, 0, W), scalar=col(l, C_KVG),
                                                          in1=fa(a_rkv, 0, W), op0=ALU.mult, op1=ALU.mult),
                  tA(a_ckv) + tA(a_rkv) + [tConst], [tB[B_CKVN]])
            cf_slots = {}
            f_zc = []
            b_zb = []
            b_zs = []
            for k in range(2):
                b = gbank()
                for j in range(31):
                    idx = k * 31 + j
                    u = idx // 10
                    if u not in cf_slots:
                        cf_slots[u] = ring_load_unit(l, U_CF + u)
                    s_ = cf_slots[u]
                    o_ = s_ * UW + (idx % 10) * 128
                    P.add("pe", lambda e, b=b, k=k, j=j, o_=o_: e.matmul(pb[b][:, 0:W], uring[:, o_:o_ + 128],
                                                                         glu[:, k * 544 + 2 + j:k * 544 + 2 + j + W], start=(j == 0), stop=(j == 30)),
                          [tUR[s_], tGlu], [tPb[b]])
                fi = F_O + 6 + k
                P.add("act", lambda e, b=b, fi=fi, k=k: e.activation(out=ft(fi, 0, W), in_=pb[b][:, 0:W], func=AF.Identity,
                                                                     bias=col(l, C_CFB + k), scale=1.0), [tPb[b], tConst], [tF[fi]])
                zb = rot("sq", B_SQ, 3)
                P.add("act", lambda e, b=b, zb=zb, k=k: e.activation(out=bt(zb, 0, W), in_=pb[b][:, 0:W], func=AF.Identity,
                                                                     bias=col(l, C_CFB + k), scale=1.0), [tPb[b], tConst], [tB[zb]])
                zs = B_G + 2 + k
                P.add("act", lambda e, b=b, zs=zs, k=k: e.activation(out=bt(zs, 0, W), in_=pb[b][:, 0:W], func=AF.Square,
                                                                     bias=col(l, C_CFB + k), scale=1.0), [tPb[b], tConst], [tB[zs]])
                f_zc.append(fi)
                b_zb.append(zb)
                b_zs.append(zs)
            if W >= 32 and g + 1 < NG:
                for k in range(2):
                    P.add("pool", lambda e, k=k: e.tensor_copy(out=glu[:, k * 544 + 2:k * 544 + 32], in_=glu[:, k * 544 + W + 2:k * 544 + W + 32]),
                          [tGlu], [tGlu])
            s_ukv2 = ring_load_unit(l, U_UKV)
            for h in range(8):
                b = gbank()
                P.add("pe", lambda e, b=b, h=h: e.matmul(pb[b][0:64, 0:W], uring[:, s_ukv2 * UW + h * 64:s_ukv2 * UW + (h + 1) * 64],
                                                         bt(B_CKVN, 0, W), start=True, stop=True), [tUR[s_ukv2], tB[B_CKVN]], [tPb[b]])
                if h % 2 == 0:
                    P.add("act", lambda e, b=b, h=h: e.copy(out=K_all[0:64, h * LT + t0:h * LT + t0 + W], in_=pb[b][0:64, 0:W]),
                          [tPb[b]], [tKh[g][h]])
                else:
                    P.add("dve", lambda e, b=b, h=h: e.tensor_copy(out=K_all[0:64, h * LT + t0:h * LT + t0 + W], in_=pb[b][0:64, 0:W]),
                          [tPb[b]], [tKh[g][h]])
            for j in range(n4):
                tw = min(128, W - 128 * j)
                b = gbank()
                P.add("pe", lambda e, b=b, j=j, tw=tw: e.matmul(pb[b][0:tw, 0:512], bt(B_CKVN, 128 * j, 128 * j + tw),
                                                                uring[:, s_ukv2 * UW + 512:s_ukv2 * UW + 1024], start=True, stop=True),
                      [tUR[s_ukv2], tB[B_CKVN]], [tPb[b]])
                kt = kt0 + j
                dst = bass.AP(V_all, kt * 520, [[33 * 520, tw], [65, 8], [1, 64]])
                src = bass.AP(pb[b], 0, [[512, tw], [64, 8], [1, 64]])
                if j % 2 == 0:
                    P.add("dve", lambda e, dst=dst, src=src: e.tensor_copy(out=dst, in_=src), [tPb[b]], [tV[g][j]])
                else:
                    P.add("act", lambda e, dst=dst, src=src: e.copy(out=dst, in_=src), [tPb[b]], [tV[g][j]])
            bS1 = 6
            bS2 = 7
            for k in range(2):
                P.add("pe", lambda e, k=k: e.matmul(pb[bS1][:, 0:W], ones, bt(b_zb[k], 0, W), start=(k == 0), stop=(k == 1)),
                      [tB[b_zb[k]], tConst], [tPb[bS1]])
            for k in range(2):
                P.add("pe", lambda e, k=k: e.matmul(pb[bS2][:, 0:W], ones, bt(b_zs[k], 0, W), start=(k == 0), stop=(k == 1)),
                      [tB[b_zs[k]], tConst], [tPb[bS2]])
            f_mu = F_O + 2
            f_t = F_O + 3
            f_rs = F_O + 4
            P.add("act", lambda e: e.mul(out=ft(f_mu, 0, W), in_=pb[bS1][:, 0:W], mul=1.0 / 256),
                  [tPb[bS1]], [tF[f_mu]])
            P.add("pool", lambda e: e.tensor_tensor(out=ft(f_t, 0, W), in0=ft(f_mu, 0, W), in1=ft(f_mu, 0, W), op=ALU.mult),
                  [tF[f_mu]], [tF[f_t]])
            P.add("dve", lambda e: e.scalar_tensor_tensor(out=ft(f_rs, 0, W), in0=pb[bS2][:, 0:W], scalar=1.0 / 256, in1=ft(f_t, 0, W),
                                                          op0=ALU.mult, op1=ALU.subtract), [tPb[bS2], tF[f_t]], [tF[f_rs]])
            for k in range(2):
                fi = f_zc[k]
                P.add("pool", lambda e, fi=fi: e.tensor_tensor(out=ft(fi, 0, W), in0=ft(fi, 0, W), in1=ft(f_mu, 0, W), op=ALU.subtract),
                      [tF[fi], tF[f_mu]], [tF[fi]])
            uq_slots = {}
            for h in range(8):
                u = h // 3
                if u not in uq_slots:
                    uq_slots[u] = ring_load_unit(l, U_UQ + u)
                s_ = uq_slots[u]
                o_ = s_ * UW + (h % 3) * 256
                bqa = gbank()
                for k in range(2):
                    P.add("pe", lambda e, k=k, bqa=bqa, o_=o_: e.matmul(pb[bqa][:, 0:W], uring[:, o_ + k * 128:o_ + (k + 1) * 128],
                                                                        bt(B_CQN + k, 0, W), start=(k == 0), stop=(k == 1)),
                          [tUR[s_], tB[B_CQN + k]], [tPb[bqa]])
                P.add("dve", lambda e, bqa=bqa, h=h: e.tensor_tensor(out=bt(B_Q + h, 0, W), in0=pb[bqa][:, 0:W], in1=cs[:, 0:W], op=ALU.mult),
                      [tPb[bqa], tCs], [tB[B_Q + h]])
            P.add("act", lambda e: e.activation(out=ft(f_rs, 0, W), in_=ft(f_rs, 0, W), func=AF.Ln, bias=EPS, scale=1.0),
                  [tF[f_rs]], [tF[f_rs]])
            P.add("act", lambda e: e.activation(out=ft(f_rs, 0, W), in_=ft(f_rs, 0, W), func=AF.Exp, scale=-0.5), [tF[f_rs]], [tF[f_rs]])
            for k in range(2):
                fi = f_zc[k]
                P.add("dve", lambda e, fi=fi: e.tensor_tensor(out=ft(fi, 0, W), in0=ft(fi, 0, W), in1=ft(f_rs, 0, W), op=ALU.mult),
                      [tF[fi], tF[f_rs]], [tF[fi]])
            s_ukv = ring_load_unit(l, U_UKV)
            for k in range(2):
                b = gbank()
                P.add("pe", lambda e, b=b, k=k: e.matmul(pb[b][:, 0:W], uring[:, s_ukv * UW + 1024 + k * 128:s_ukv * UW + 1152 + k * 128],
                                                         bt(p_tiles[k], 0, W), start=True, stop=True),
                      [tUR[s_ukv], tB[p_tiles[k]]], [tPb[b]])
                P.add("dve", lambda e, b=b, k=k: e.scalar_tensor_tensor(out=bt(B_U + k, 0, W), in0=pb[b][:, 0:W], scalar=col(l, C_PSC + k),
                                                                        in1=bt(g_pg[k], 0, W), op0=ALU.mult, op1=ALU.mult),
                      [tPb[b], tB[g_pg[k]], tConst], [tB[B_U + k]])
            s_sc = ring_load_unit(l, U_SC)
            for k in range(2):
                b = gbank()
                for j in range(3):
                    P.add("pe", lambda e, b=b, k=k, j=j: e.matmul(pb[b][:, 0:W], uring[:, s_sc * UW + (k * 3 + j) * 128:s_sc * UW + (k * 3 + j + 1) * 128],
                                                                  prodb[:, k * 520 + j:k * 520 + j + W], start=(j == 0), stop=(j == 2)),
                          [tUR[s_sc], tProd], [tPb[b]])
                P.add("dve", lambda e, b=b, k=k: e.tensor_tensor(out=bt(B_U + 8 + k, 0, W), in0=pb[b][:, 0:W], in1=ft(f_bs[k], 0, W), op=ALU.mult),
                      [tPb[b], tF[f_bs[k]]], [tB[B_U + 8 + k]])
            if W >= 2 and g + 1 < NG:
                for k in range(2):
                    P.add("pool", lambda e, k=k: e.tensor_copy(out=prodb[:, k * 520:k * 520 + 2], in_=prodb[:, k * 520 + W:k * 520 + W + 2]),
                          [tProd], [tProd])
            g_cg = []
            for k in range(2):
                b = zmm(l, VC_CG + k, W)
                gi = B_G + k
                P.add("act", lambda e, b=b, gi=gi: e.activation(out=bt(gi, 0, W), in_=pb[b][:, 0:W], func=AF.Silu),
                      [tPb[b]], [tB[gi]])
                g_cg.append(gi)
            for k in range(2):
                fi = f_zc[k]
                P.add("act", lambda e, fi=fi, k=k: e.activation(out=ft(fi, 0, W), in_=ft(fi, 0, W), func=AF.Silu,
                                                                bias=col(l, C_LNB + k), scale=col(l, C_LNG + k)), [tF[fi], tConst], [tF[fi]])
                P.add("dve", lambda e, fi=fi, k=k: e.tensor_tensor(out=bt(B_U + 6 + k, 0, W), in0=ft(fi, 0, W), in1=bt(g_cg[k], 0, W), op=ALU.mult),
                      [tF[fi], tB[g_cg[k]]], [tB[B_U + 6 + k]])
            g_mgs = [B_G + 0, B_P + 0, B_P + 1, B_CQN + 0]
            for p in range(4):
                b = zmm(l, VC_MG + p, W)
                P.add("act", lambda e, b=b, p=p: e.activation(out=bt(g_mgs[p], 0, W), in_=pb[b][:, 0:W], func=AF.Silu), [tPb[b]], [tB[g_mgs[p]]])
            if g + 1 < NG:
                prenorm1(l, g + 1)
            nkt = kt0 + n4
            gstate["att"] = True
            cnt["gb"] = 0
            pending = [None]
            f_x = F_O + 0
            f_u = F_O + 1
            f_r = F_O + 2

            def norm_tail(p):
                g_mg = g_mgs[p]
                br = gbank()
                P.add("pe", lambda e: e.matmul(pb[br][:, 0:W], sel, ft(f_x, 0, W), start=True, stop=True), [tF[f_x], tConst], [tPb[br]])
                P.add("dve", lambda e: e.reciprocal(out=ft(f_r, 0, W), in_=pb[br][:, 0:W]), [tPb[br]], [tF[f_r]])
                P.add("pool", lambda e: e.tensor_tensor(out=ft(f_u, 0, W), in0=ft(f_u, 0, W), in1=ft(f_r, 0, W), op=ALU.mult),
                      [tF[f_u], tF[f_r]], [tF[f_u]])
                P.add("pool", lambda e: e.tensor_tensor(out=bt(B_U + 2 + p, 0, W), in0=ft(f_u, 0, W), in1=bt(g_mg, 0, W), op=ALU.mult),
                      [tF[f_u], tB[g_mg]], [tB[B_U + 2 + p]])

            for p in range(4):
                items = [(hh, kt) for hh in range(2) for kt in range(nkt)]
                n_it = len(items)

                def geom(i):
                    hh, kt = items[i]
                    h = 2 * p + hh
                    if kt < kt0:
                        return h, hh, kt, 0, 128
                    c0 = 128 * (kt - kt0)
                    return h, hh, kt, c0, min(128, W - c0)

                def qk(i):
                    h, hh, kt, c0, kw = geom(i)
                    sbk = 3 + i % 3
                    kdeps = [tKh[gg][h] for gg in range(g + 1)] + tKr[:g + 1]
                    diag = kt >= kt0
                    P.add("pe", lambda e: e.matmul(pb[sbk][0:kw, c0:W], K_all[:, h * LT + kt * 128:h * LT + kt * 128 + kw],
                                                   bt(B_Q + h, c0, W), start=True, stop=(not diag)),
                          kdeps + [tB[B_Q + h]], [tPb[sbk]])
                    if diag:
                        P.add("pe", lambda e: e.matmul(pb[sbk][0:kw, c0:c0 + kw], cbf[0:kw, 0:kw], cbf[0:kw, 256:256 + kw], start=False, stop=True),
                              [tConst], [tPb[sbk]])
                    pi = B_G + 1 + i % 3
                    P.add("act", lambda e: e.activation(out=bt(pi, c0, W, 0, kw), in_=pb[sbk][0:kw, c0:W], func=AF.Exp, scale=SCALE),
                          [tPb[sbk]], [tB[pi]])

                def pv(i):
                    h, hh, kt, c0, kw = geom(i)
                    ob = 6 + hh
                    vs = h * 65 if hh == 0 else (h - 1) * 65 + 1
                    pi = B_G + 1 + i % 3
                    vdeps = [t for gg in range(g + 1) for t in tV[gg]]
                    P.add("pe", lambda e: e.matmul(pb[ob][:, c0:W], V_all[0:kw, kt * 520 + vs:kt * 520 + vs + 128], bt(pi, c0, W, 0, kw),
                                                   start=(kt == 0), stop=(kt == nkt - 1)), vdeps + [tB[pi]], [tPb[ob]])

                if W <= 16 and kt0 * W <= 512:
                    if pending[0] is not None:
                        norm_tail(pending[0])
                        pending[0] = None
                    for hh in range(2):
                        h = 2 * p + hh
                        sbk = 3 + hh
                        pi = B_G + 1 + hh
                        kdeps = [tKh[gg][h] for gg in range(g + 1)] + tKr[:g + 1]
                        for kt in range(kt0):
                            P.add("pe", lambda e, h=h, kt=kt, sbk=sbk: e.matmul(pb[sbk][:, kt * W:(kt + 1) * W], K_all[:, h * LT + kt * 128:h * LT + kt * 128 + 128],
                                                                             bt(B_Q + h, 0, W), start=True, stop=True),
                                  kdeps + [tB[B_Q + h]], [tPb[sbk]])
                        if kt0 > 0:
                            P.add("act", lambda e, sbk=sbk, pi=pi: e.activation(out=bt(pi, 0, kt0 * W), in_=pb[sbk][:, 0:kt0 * W], func=AF.Exp, scale=SCALE),
                                  [tPb[sbk]], [tB[pi]])
                        dc = hh * W
                        P.add("pe", lambda e, h=h, dc=dc: e.matmul(pb[5][0:W, dc:dc + W], K_all[:, h * LT + t0:h * LT + t0 + W],
                                                                 bt(B_Q + h, 0, W), start=True, stop=False),
                              kdeps + [tB[B_Q + h]], [tPb[5]])
                        P.add("pe", lambda e, dc=dc: e.matmul(pb[5][0:W, dc:dc + W], cbf[0:W, 0:W], cbf[0:W, 256:256 + W], start=False, stop=True),
                              [tConst], [tPb[5]])
                        P.add("act", lambda e, dc=dc: e.activation(out=bt(B_G + 3, dc, dc + W, 0, W), in_=pb[5][0:W, dc:dc + W], func=AF.Exp, scale=SCALE),
                              [tPb[5]], [tB[B_G + 3]])
                    for hh in range(2):
                        h = 2 * p + hh
                        ob = 6 + hh
                        pi = B_G + 1 + hh
                        dc = hh * W
                        vs = h * 65 if hh == 0 else (h - 1) * 65 + 1
                        vdeps = [t for gg in range(g + 1) for t in tV[gg]]
                        for kt in range(kt0):
                            P.add("pe", lambda e, ob=ob, pi=pi, kt=kt, vs=vs: e.matmul(pb[ob][:, 0:W], V_all[:, kt * 520 + vs:kt * 520 + vs + 128],
                                                                                    bt(pi, kt * W, (kt + 1) * W), start=(kt == 0), stop=False),
                                  vdeps + [tB[pi]], [tPb[ob]])
                        P.add("pe", lambda e, ob=ob, dc=dc, vs=vs: e.matmul(pb[ob][:, 0:W], V_all[0:W, kt0 * 520 + vs:kt0 * 520 + vs + 128],
                                                                         bt(B_G + 3, dc, dc + W, 0, W), start=(kt0 == 0), stop=True),
                              vdeps + [tB[B_G + 3]], [tPb[ob]])
                else:
                    qk(0)
                    if n_it > 1:
                        qk(1)
                    if pending[0] is not None:
                        norm_tail(pending[0])
                        pending[0] = None
                    for i in range(n_it):
                        if i + 2 < n_it:
                            qk(i + 2)
                        pv(i)
                P.add("dve", lambda e: e.tensor_copy(out=ft(f_x, 0, W, 0, 64), in_=pb[7][0:64, 0:W]), [tPb[7]], [tF[f_x]])
                P.add("act", lambda e: e.copy(out=ft(f_x, 0, W, 64, 128), in_=pb[6][64:128, 0:W]), [tPb[6]], [tF[f_x]])
                P.add("dve", lambda e: e.tensor_copy(out=ft(f_u, 0, W, 0, 64), in_=pb[6][0:64, 0:W]), [tPb[6]], [tF[f_u]])
                P.add("act", lambda e: e.copy(out=ft(f_u, 0, W, 64, 128), in_=pb[7][64:128, 0:W]), [tPb[7]], [tF[f_u]])
                pending[0] = p
            norm_tail(pending[0])
            gstate["att"] = False
            cnt["gb"] = 0
            kch = [(0, 2), (2, 6), (6, 8), (8, 10)]
            for c in range(8):
                s_op = ring_load_unit(l, U_OP + c)
                f_acc = F_X + c % 2
                f_tmp = F_X + 2 + c % 2
                for i in range(4):
                    b = zmm(l, VC_GL + c * 4 + i, W)
                    gi = B_G + i
                    P.add("act", lambda e, b=b, gi=gi, c=c, i=i: e.activation(out=bt(gi, 0, W), in_=pb[b][:, 0:W], func=AF.Sigmoid,
                                                                              bias=col(l, C_GBIAS + c * 4 + i), scale=1.0), [tPb[b], tConst], [tB[gi]])
                    by = gbank()
                    k0, k1 = kch[i]
                    for k in range(k0, k1):
                        P.add("pe", lambda e, by=by, k=k, k0=k0, k1=k1, s_op=s_op: e.matmul(pb[by][:, 0:W], uring[:, s_op * UW + k * 128:s_op * UW + (k + 1) * 128],
                                                                                 bt(B_U + k, 0, W), start=(k == k0), stop=(k == k1 - 1)),
                              [tUR[s_op], tB[B_U + k]], [tPb[by]])
                    if i == 0:
                        P.add("dve", lambda e, by=by, gi=gi, f_acc=f_acc: e.tensor_tensor(out=ft(f_acc, 0, W), in0=pb[by][:, 0:W], in1=bt(gi, 0, W), op=ALU.mult),
                              [tPb[by], tB[gi]], [tF[f_acc]])
                    else:
                        P.add("dve", lambda e, by=by, gi=gi, f_tmp=f_tmp: e.tensor_tensor(out=ft(f_tmp, 0, W), in0=pb[by][:, 0:W], in1=bt(gi, 0, W), op=ALU.mult),
                              [tPb[by], tB[gi]], [tF[f_tmp]])
                        if i < 3:
                            P.add("pool", lambda e, f_acc=f_acc, f_tmp=f_tmp: e.tensor_tensor(out=ft(f_acc, 0, W), in0=ft(f_acc, 0, W), in1=ft(f_tmp, 0, W), op=ALU.add),
                                  [tF[f_acc], tF[f_tmp]], [tF[f_acc]])
                        else:
                            P.add("pool", lambda e, c=c, f_acc=f_acc, f_tmp=f_tmp: e.tensor_tensor(out=bt(B_Q + c, 0, W), in0=ft(f_acc, 0, W), in1=ft(f_tmp, 0, W), op=ALU.add),
                                  [tF[f_acc], tF[f_tmp]], [tB[B_Q + c]])
            if debug:
                for i in range(10):
                    P.add("sp", lambda e, i=i: e.dma_start(out=dbg_d[g, :, i * 512:i * 512 + W], in_=bt(B_U + i, 0, W)),
                          [tB[B_U + i]], [], lane=tB[B_U + i].lane)
                for i in range(8):
                    P.add("sp", lambda e, i=i: e.dma_start(out=dbg_d[g, :, (10 + i) * 512:(10 + i) * 512 + W], in_=bt(B_Q + i, 0, W)),
                          [tB[B_Q + i]], [], lane=tB[B_Q + i].lane)
            cnt["x"] = 0
            if g + 1 < NG:
                make_hT(l, g + 1)
            sq_prev = None
            for c in range(8):
                s_wo = ring_load_unit(l, U_WO + c)
                b = gbank()
                for k in range(8):
                    P.add("pe", lambda e, b=b, k=k, s_wo=s_wo: e.matmul(pb[b][:, 0:W], uring[:, s_wo * UW + k * 128:s_wo * UW + (k + 1) * 128],
                                                             bt(B_Q + k, 0, W), start=(k == 0), stop=(k == 7)),
                          [tUR[s_wo], tB[B_Q + k]], [tPb[b]])
                if sq_prev is not None:
                    P.add("pe", lambda e, sq=sq_prev, c=c: e.matmul(pb[ST][:, 0:W], ones, bt(sq, 0, W), start=(c == 1), stop=False),
                          [tB[sq_prev], tConst], [tPb[ST]])
                sq = rot("sq", B_SQ, 3)
                P.add("act", lambda e, b=b, sq=sq: e.activation(out=bt(sq, 0, W), in_=pb[b][:, 0:W], func=AF.Square), [tPb[b]], [tB[sq]])
                P.add("act", lambda e, b=b, c=c: e.copy(out=ft(F_O + c, 0, W), in_=pb[b][:, 0:W]), [tPb[b]], [tF[F_O + c]])
                sq_prev = sq
            P.add("pe", lambda e, sq=sq_prev: e.matmul(pb[ST][:, 0:W], ones, bt(sq, 0, W), start=False, stop=True),
                  [tB[sq_prev], tConst], [tPb[ST]])
            f_rp = F_X + 3
            rstd_from_stat(ST, W, 1.0 / D, f_rp)
            pending_final[0] = (l, g)
            cnt["x"] = 0

        for l in range(n_layers):
            P.add("pool", lambda e: e.memset(pvbuf[:], 0.0), [], [tPv])
            P.add("pool", lambda e: e.memset(glu[:], 0.0), [], [tGlu])
            P.add("pool", lambda e: e.memset(prodb[:], 0.0), [], [tProd])
            if l + 1 < n_layers:
                cast_q.extend([(l + 1, "i", v) for v in range(NVC)] + [(l + 1, "u", u) for u in range(NU)])
            prenorm1(l, 0)
            cnt["x"] = 0
            make_hT(l, 0)
            cnt["x"] = 0
            for g in range(NG):
                layer_group(l, g)
            while cast_q:
                emit_cast_chunk(*cast_q.pop(0))
            if pending_final[0] is not None:
                emit_final(*pending_final[0])
                pending_final[0] = None

        def semctx(name):
            return es.enter_context(nc.semaphore(name))

        P.finalize(semctx)
        with nc.Block() as block:
            @block.sync
            def _(e):
                P.emit("sp", e)

            @block.tensor
            def _(e):
                P.emit("pe", e)

            @block.scalar
            def _(e):
                P.emit("act", e)

            @block.vector
            def _(e):
                P.emit("dve", e)

            @block.gpsimd
            def _(e):
                P.emit("pool", e)
    return nc


def _prep_weights(inp):
    f32 = np.float32
    w_in = np.asarray(inp["w_in"], f32)
    win = np.zeros((NL, NVC, 128, 8, 128), f32)

    def put(l, vc, cols_src, dst0=0):
        blk = w_in[l][:, cols_src]
        n = blk.shape[1]
        win[l, vc, :, :, dst0:dst0 + n] = blk.reshape(8, 128, n).transpose(1, 0, 2)

    perm = np.concatenate([np.arange(16, 32), np.arange(0, 16)])
    for l in range(NL):
        for k in range(2):
            put(l, VC_PV + k, np.arange(0 + 128 * k, 128 * (k + 1)))
            put(l, VC_PG + k, np.arange(256 + 128 * k, 256 + 128 * (k + 1)))
            put(l, VC_CQ + k, np.arange(512 + 128 * k, 512 + 128 * (k + 1)))
            put(l, VC_CA + k, np.arange(1440 + 128 * k, 1440 + 128 * (k + 1)))
            put(l, VC_CGATE + k, np.arange(1696 + 128 * k, 1696 + 128 * (k + 1)))
            put(l, VC_CG + k, np.arange(1952 + 128 * k, 1952 + 128 * (k + 1)))
            put(l, VC_BG + k, np.arange(2208 + 128 * k, 2208 + 128 * (k + 1)))
            put(l, VC_CGS + k, np.arange(2464 + 128 * k, 2464 + 128 * (k + 1)))
            put(l, VC_XV + k, np.arange(2720 + 128 * k, 2720 + 128 * (k + 1)))
            put(l, VC_SG + k, np.arange(2976 + 128 * k, 2976 + 128 * (k + 1)))
        put(l, VC_CKV, np.arange(768, 896))
        put(l, VC_KRA, np.arange(896, 928), 64)
        put(l, VC_KRA, 896 + perm, 96)
        put(l, VC_KRB, 896 + perm, 64)
        put(l, VC_KRB, np.arange(896, 928), 96)
        for p in range(4):
            put(l, VC_MG + p, np.arange(928 + 128 * p, 928 + 128 * (p + 1)))
        for c in range(8):
            for i in range(4):
                put(l, VC_GL + c * 4 + i, np.arange(3232 + 1024 * i + 128 * c, 3232 + 1024 * i + 128 * (c + 1)))
    win_f = np.ascontiguousarray(win.transpose(0, 2, 1, 3, 4).reshape(NL * 128, NVC * 1024))

    wun = np.zeros((NL, 128, NU, UW), f32)
    pool_w = np.asarray(inp["pool_w"], f32)
    w_uq = np.asarray(inp["w_uq"], f32)
    w_ukv = np.asarray(inp["w_ukv"], f32)
    cf_w = np.asarray(inp["conf_dw_w"], f32)
    sc_w = np.asarray(inp["sc_dw_w"], f32)
    w_outs = [np.asarray(inp[k], f32) for k in ("w_out_pool", "w_out_mla", "w_out_conf", "w_out_sc")]
    w_o = np.asarray(inp["w_o"], f32)
    ar = np.arange(128)
    for l in range(NL):
        kv = w_ukv[l].reshape(128, 8, 128)
        wun[l, :, U_UKV, 0:512] = kv[:, :, 0:64].reshape(128, 512)
        wun[l, :, U_UKV, 512:1024] = kv[:, :, 64:128].reshape(128, 512)
        for k in range(2):
            bd = np.zeros((128, 128), f32)
            bd[0:64, 0:64] = pool_w[l, 2 * k]
            bd[64:128, 64:128] = pool_w[l, 2 * k + 1]
            wun[l, :, U_UKV, 1024 + 128 * k:1152 + 128 * k] = bd
        for h in range(8):
            qa = w_uq[l][:, 96 * h:96 * h + 96]
            qx = np.concatenate([qa, qa[:, 64 + perm]], axis=1)
            u, o_ = U_UQ + h // 3, (h % 3) * 256
            for k in range(2):
                wun[l, :, u, o_ + k * 128:o_ + (k + 1) * 128] = qx[128 * k:128 * (k + 1)]
        for k in range(2):
            for j in range(31):
                idx = k * 31 + j
                dm = np.zeros((128, 128), f32)
                dm[ar, ar] = cf_w[l, j, 128 * k:128 * (k + 1)]
                wun[l, :, U_CF + idx // 10, (idx % 10) * 128:(idx % 10 + 1) * 128] = dm
            for j in range(3):
                idx = k * 3 + j
                dm = np.zeros((128, 128), f32)
                dm[ar, ar] = sc_w[l, j, 128 * k:128 * (k + 1)]
                wun[l, :, U_SC, idx * 128:(idx + 1) * 128] = dm
        wcat = np.concatenate([w[l] for w in w_outs], axis=0)
        for c in range(8):
            blk = wcat[:, 128 * c:128 * (c + 1)].reshape(10, 128, 128).transpose(1, 0, 2).reshape(128, 1280)
            wun[l, :, U_OP + c, :] = blk
            blk = w_o[l][:, 128 * c:128 * (c + 1)].reshape(8, 128, 128).transpose(1, 0, 2).reshape(128, 1024)
            wun[l, :, U_WO + c, 0:1024] = blk
    wun_f = np.ascontiguousarray(wun.reshape(NL * 128, NU * UW))

    cols = np.zeros((128, NL, NCOLL), f32)
    for l in range(NL):
        cols[:, l, C_GPRE:C_GPRE + 8] = np.asarray(inp["pre_norm_g"], f32)[l].reshape(8, 128).T
        cols[:, l, C_GPOST:C_GPOST + 8] = np.asarray(inp["post_norm_g"], f32)[l].reshape(8, 128).T
        gb = np.asarray(inp["gate_bias"], f32)[l].reshape(4, 8, 128)
        cols[:, l, C_GBIAS:C_GBIAS + 32] = gb.transpose(2, 1, 0).reshape(128, 32)
        cols[:, l, C_PSC:C_PSC + 2] = np.asarray(inp["pool_scale"], f32)[l].reshape(2, 128).T
        cols[:, l, C_QG:C_QG + 2] = np.asarray(inp["q_norm_g"], f32)[l].reshape(2, 128).T
        cols[:, l, C_KVG] = np.asarray(inp["kv_norm_g"], f32)[l]
        cols[:, l, C_CFB:C_CFB + 2] = np.asarray(inp["conf_dw_b"], f32)[l].reshape(2, 128).T
        cols[:, l, C_LNG:C_LNG + 2] = np.asarray(inp["conf_ln_g"], f32)[l].reshape(2, 128).T
        cols[:, l, C_LNB:C_LNB + 2] = np.asarray(inp["conf_ln_b"], f32)[l].reshape(2, 128).T
    cols = np.ascontiguousarray(cols.reshape(128, NL * NCOLL))
    return win_f, wun_f, cols


def _consts():
    f32 = np.float32
    cbf = np.zeros((128, 384), f32)
    cbf[:, 0:128] = np.eye(128, dtype=f32)
    cbf[:, 128:256] = 1.0
    k = np.arange(128)[:, None]
    q = np.arange(128)[None, :]
    cbf[:, 256:384] = np.where(k > q, -30000.0, 0.0)
    cf32 = np.zeros((128, 160), f32)
    cf32[64, 0:64] = 1.0
    cf32[63, 64:128] = 1.0
    wins = {0: (2, 4), 1: (8, 16)}
    t = np.arange(16)
    for kk in range(2):
        for half in range(2):
            w = wins[kk][half]
            cf32[64 * half:64 * half + 64, 128 + 16 * kk:144 + 16 * kk] = 1.0 / np.minimum(t + 1, w).astype(f32)
    inv = 1.0 / (10000.0 ** (np.arange(0, 32, 2, dtype=f32) / 32))
    ang = np.arange(LT, dtype=f32)[:, None] * inv[None, :]
    cos = np.cos(ang).astype(f32).T
    sin = np.sin(ang).astype(f32).T
    cs = np.zeros((128, 2, LT), f32)
    sgn = np.concatenate([-sin, sin], axis=0)
    cc = np.concatenate([cos, cos], axis=0)
    cs[0:64, 0] = 1.0
    cs[64:96, 0] = cc
    cs[96:128, 0] = sgn
    cs[64:96, 1] = sgn
    cs[96:128, 1] = cc
    return cbf, cf32, np.ascontiguousarray(cs.reshape(128, 2 * LT))


_CACHE = {}


def kernel(**inp):
    x = np.asarray(inp["x"], np.float32)
    meta = np.asarray(inp["meta_tokens"], np.float32)
    B = x.shape[0]
    win_f, wun_f, cols = _prep_weights(inp)
    cbf, cf32, cs = _consts()
    if "nc" not in _CACHE:
        _CACHE["nc"] = build_nc()
    nc = _CACHE["nc"]
    in_maps = []
    for b in range(B):
        xin = np.ascontiguousarray(np.concatenate([meta, x[b]], axis=0).T)
        in_maps.append({"xin": xin, "win_f": win_f, "wun_f": wun_f, "cols": cols, "cbf": cbf, "cf32": cf32, "cs": cs})
    res = run_bass_kernel_spmd(nc, in_maps, core_ids=list(range(B)))
    out = np.empty((B, SEQ, D), np.float32)
    for b in range(B):
        out[b] = res.results[b]["y"][:, NMETA:].T
    return out
```
